# Optimizing a Trainium2 kernel written in Bass

```python
import math
import jax, jax.numpy as jnp
from jax import lax
import numpy as np

D_MODEL = 2048
BATCH = 4
SEQ = 4096
DEPTH = 1

HEAD_DIM = 128
N_HEADS_TOTAL = D_MODEL // HEAD_DIM
MLA_HEADS = N_HEADS_TOTAL // 2
MOBA_HEADS = N_HEADS_TOTAL - MLA_HEADS
MLA_Q_RANK = 3 * D_MODEL // 16
MLA_KV_RANK = D_MODEL // 8
MLA_NOPE_DIM = HEAD_DIM
MLA_ROPE_DIM = HEAD_DIM // 2
MLA_V_DIM = HEAD_DIM
MOBA_BLOCK = 256
MOBA_TOPK = 3
MOBA_Q_CHUNK = 16
ATTN_Q_BLOCK = 128
ROPE_THETA = 500000.0
PARTIAL_ROT_DIM = HEAD_DIM // 4
D_FF = 4 * D_MODEL
LN_EPS = 1e-5
RMS_EPS = 1e-6
DEEPNORM_ALPHA = (2 * DEPTH) ** 0.25
DEEPNORM_BETA = (8 * DEPTH) ** -0.25
MOBA_WIDTH = MOBA_HEADS * HEAD_DIM
IN_SPLITS = (MLA_Q_RANK, MLA_KV_RANK, MLA_ROPE_DIM, MOBA_WIDTH, MOBA_WIDTH, MOBA_WIDTH)
D_IN = sum(IN_SPLITS)
MLA_SCALE = 1.0 / math.sqrt(MLA_NOPE_DIM + MLA_ROPE_DIM)
MOBA_SCALE = 1.0 / math.sqrt(HEAD_DIM)

kernel_name = "hybrid_mla_moba_deepnorm_layer"


def layer_norm(x, g, b):
    xf = x.astype(jnp.float32)
    mu = jnp.mean(xf, axis=-1, keepdims=True)
    var = jnp.mean(jnp.square(xf - mu), axis=-1, keepdims=True)
    y = (xf - mu) * lax.rsqrt(var + LN_EPS) * g.astype(jnp.float32) + b.astype(jnp.float32)
    return y.astype(x.dtype)


def rms_norm(x, g):
    xf = x.astype(jnp.float32)
    y = xf * lax.rsqrt(jnp.mean(jnp.square(xf), axis=-1, keepdims=True) + RMS_EPS)
    return (y * g.astype(jnp.float32)).astype(x.dtype)


def apply_rope(x, positions):
    d = x.shape[-1]
    half = d // 2
    inv_freq = ROPE_THETA ** (-jnp.arange(half, dtype=jnp.float32) * (2.0 / d))
    ang = positions.astype(jnp.float32)[:, :, None, None] * inv_freq
    cos, sin = jnp.cos(ang), jnp.sin(ang)
    xf = x.astype(jnp.float32)
    x1, x2 = xf[..., :half], xf[..., half:]
    return jnp.concatenate([x1 * cos - x2 * sin, x2 * cos + x1 * sin], axis=-1).astype(x.dtype)


def causal_dense_attention(q, k, v, scale):
    B, S, H, _ = q.shape
    dv = v.shape[-1]
    q, k, v = (t.transpose(0, 2, 1, 3) for t in (q, k, v))
    kpos = jnp.arange(S)

    def one_block(i):
        start = i * ATTN_Q_BLOCK
        qb = lax.dynamic_slice_in_dim(q, start, ATTN_Q_BLOCK, axis=2)
        s = jnp.einsum('bhqd,bhkd->bhqk', qb, k, preferred_element_type=jnp.float32) * scale
        qpos = start + jnp.arange(ATTN_Q_BLOCK)
        s = jnp.where(kpos[None, :] <= qpos[:, None], s, -jnp.inf)
        p = jax.nn.softmax(s, axis=-1).astype(v.dtype)
        return jnp.einsum('bhqk,bhkd->bhqd', p, v)

    out = lax.map(one_block, jnp.arange(S // ATTN_Q_BLOCK))
    return out.transpose(1, 0, 3, 2, 4).reshape(B, S, H * dv)


def mla_group(c_q, c_kv, k_rope_in, positions, q_norm, kv_norm, w_uq, w_ukv):
    B, S, _ = c_q.shape
    q = (rms_norm(c_q, q_norm) @ w_uq).reshape(B, S, MLA_HEADS, MLA_NOPE_DIM + MLA_ROPE_DIM)
    q_nope, q_rope = q[..., :MLA_NOPE_DIM], q[..., MLA_NOPE_DIM:]
    q = jnp.concatenate([q_nope, apply_rope(q_rope, positions)], axis=-1)
    kv = (rms_norm(c_kv, kv_norm) @ w_ukv).reshape(B, S, MLA_HEADS, MLA_NOPE_DIM + MLA_V_DIM)
    k_nope, v = kv[..., :MLA_NOPE_DIM], kv[..., MLA_NOPE_DIM:]
    k_rope = apply_rope(k_rope_in[:, :, None, :], positions)
    k = jnp.concatenate([k_nope, jnp.broadcast_to(k_rope, (B, S, MLA_HEADS, MLA_ROPE_DIM))], axis=-1)
    return causal_dense_attention(q, k, v, MLA_SCALE)


def partial_rope(x, positions):
    return jnp.concatenate([apply_rope(x[..., :PARTIAL_ROT_DIM], positions), x[..., PARTIAL_ROT_DIM:]], axis=-1)


def moba_group(q, k, v, positions):
    B, S, H, D = q.shape
    L = MOBA_BLOCK
    q = partial_rope(q, positions).transpose(0, 2, 1, 3)
    k = partial_rope(k, positions).transpose(0, 2, 1, 3)
    v = v.transpose(0, 2, 1, 3)
    n_blocks = -(-S // L)
    s_pad = n_blocks * L
    pad = ((0, 0), (0, 0), (0, s_pad - S), (0, 0))
    k_pad, v_pad = jnp.pad(k, pad), jnp.pad(v, pad)
    kb = k_pad.reshape(B, H, n_blocks, L, D)
    vb = v_pad.reshape(B, H, n_blocks, L, D)
    k_mean = jnp.mean(kb.astype(jnp.float32), axis=3).astype(q.dtype)
    gate = jnp.einsum('bhsd,bhnd->bhsn', q, k_mean, preferred_element_type=jnp.float32)
    q_blk = jnp.arange(S) // L
    past = jnp.arange(n_blocks)[None, :] < q_blk[:, None]
    gate = jnp.where(past, gate, -jnp.inf)
    topk = min(MOBA_TOPK, n_blocks)
    _, sel_idx = lax.top_k(gate, topk)
    sel_valid = sel_idx < q_blk[:, None]
    gather_blocks = jax.vmap(jax.vmap(lambda blocks, ix: blocks[ix]))

    def one_chunk(c):
        start = c * MOBA_Q_CHUNK
        qc = lax.dynamic_slice_in_dim(q, start, MOBA_Q_CHUNK, axis=2)
        ic = lax.dynamic_slice_in_dim(sel_idx, start, MOBA_Q_CHUNK, axis=2)
        vc = lax.dynamic_slice_in_dim(sel_valid, start, MOBA_Q_CHUNK, axis=2)
        k_sel = gather_blocks(kb, ic)
        v_sel = gather_blocks(vb, ic)
        s_sel = jnp.einsum('bhqd,bhqjld->bhqjl', qc, k_sel, preferred_element_type=jnp.float32) * MOBA_SCALE
        s_sel = jnp.where(vc[..., None], s_sel, -jnp.inf).reshape(B, H, MOBA_Q_CHUNK, topk * L)
        own_start = (start // L) * L
        k_own = lax.dynamic_slice_in_dim(k_pad, own_start, L, axis=2)
        v_own = lax.dynamic_slice_in_dim(v_pad, own_start, L, axis=2)
        s_own = jnp.einsum('bhqd,bhld->bhql', qc, k_own, preferred_element_type=jnp.float32) * MOBA_SCALE
        qpos = start + jnp.arange(MOBA_Q_CHUNK)
        kpos = own_start + jnp.arange(L)
        s_own = jnp.where(kpos[None, :] <= qpos[:, None], s_own, -jnp.inf)
        p = jax.nn.softmax(jnp.concatenate([s_sel, s_own], axis=-1), axis=-1).astype(v.dtype)
        p_sel = p[..., :topk * L].reshape(B, H, MOBA_Q_CHUNK, topk, L)
        p_own = p[..., topk * L:]
        return (jnp.einsum('bhqjl,bhqjld->bhqd', p_sel, v_sel)
                + jnp.einsum('bhql,bhld->bhqd', p_own, v_own))

    out = lax.map(one_chunk, jnp.arange(S // MOBA_Q_CHUNK))
    return out.transpose(1, 0, 3, 2, 4).reshape(B, S, H * D)


def setup_inputs(seed: int = 0) -> dict:
    key = jax.random.key(seed)
    ks = jax.random.split(key, 16)
    f32 = jnp.float32
    nrm = lambda k, shape, fan_in, gain=1.0: jax.random.normal(k, shape, f32) * (gain * fan_in ** -0.5)
    x = jax.random.normal(ks[0], (BATCH, SEQ, D_MODEL), f32)
    offsets = jax.random.randint(ks[1], (BATCH, 1), 0, 1024, dtype=jnp.int32)
    positions = (offsets + jnp.arange(SEQ, dtype=jnp.int32)[None, :]).astype(jnp.int32)
    w_in = nrm(ks[2], (DEPTH, D_MODEL, D_IN), D_MODEL)
    mla_q_norm = 1.0 + 0.02 * jax.random.normal(ks[3], (DEPTH, MLA_Q_RANK), f32)
    mla_kv_norm = 1.0 + 0.02 * jax.random.normal(ks[4], (DEPTH, MLA_KV_RANK), f32)
    w_uq = nrm(ks[5], (DEPTH, MLA_Q_RANK, MLA_HEADS * (MLA_NOPE_DIM + MLA_ROPE_DIM)), MLA_Q_RANK)
    w_ukv = nrm(ks[6], (DEPTH, MLA_KV_RANK, MLA_HEADS * (MLA_NOPE_DIM + MLA_V_DIM)), MLA_KV_RANK)
    w_out = nrm(ks[7], (DEPTH, MLA_HEADS * MLA_V_DIM + MOBA_WIDTH, D_MODEL), D_MODEL, DEEPNORM_BETA)
    ln1_g = 1.0 + 0.02 * jax.random.normal(ks[8], (DEPTH, D_MODEL), f32)
    ln1_b = 0.02 * jax.random.normal(ks[9], (DEPTH, D_MODEL), f32)
    w_up = nrm(ks[10], (DEPTH, D_MODEL, D_FF), D_MODEL)
    w_down = nrm(ks[11], (DEPTH, D_FF, D_MODEL), D_FF, DEEPNORM_BETA)
    ln2_g = 1.0 + 0.02 * jax.random.normal(ks[12], (DEPTH, D_MODEL), f32)
    ln2_b = 0.02 * jax.random.normal(ks[13], (DEPTH, D_MODEL), f32)
    return {"x": x, "positions": positions, "w_in": w_in, "mla_q_norm": mla_q_norm,
            "mla_kv_norm": mla_kv_norm, "w_uq": w_uq, "w_ukv": w_ukv, "w_out": w_out,
            "ln1_g": ln1_g, "ln1_b": ln1_b, "w_up": w_up, "w_down": w_down,
            "ln2_g": ln2_g, "ln2_b": ln2_b}


def reference(x, positions, w_in, mla_q_norm, mla_kv_norm, w_uq, w_ukv, w_out,
              ln1_g, ln1_b, w_up, w_down, ln2_g, ln2_b):
    B, S, _ = x.shape
    cuts = list(np.cumsum(IN_SPLITS)[:-1])
    for l in range(DEPTH):
        h = x @ w_in[l]
        c_q, c_kv, k_r, m_q, m_k, m_v = jnp.split(h, cuts, axis=-1)
        mla_out = mla_group(c_q, c_kv, k_r, positions, mla_q_norm[l], mla_kv_norm[l], w_uq[l], w_ukv[l])
        rs = lambda t: t.reshape(B, S, MOBA_HEADS, HEAD_DIM)
        moba_out = moba_group(rs(m_q), rs(m_k), rs(m_v), positions)
        mix = jnp.concatenate([mla_out, moba_out], axis=-1) @ w_out[l]
        x = layer_norm(DEEPNORM_ALPHA * x + mix, ln1_g[l], ln1_b[l])
        ff = jnp.square(jax.nn.relu(x @ w_up[l])) @ w_down[l]
        x = layer_norm(DEEPNORM_ALPHA * x + ff, ln2_g[l], ln2_b[l])
    return x
```

```python
import math
from contextlib import ExitStack

import numpy as np
import ml_dtypes

import concourse.bass as bass
import concourse.mybir as mybir
from concourse.bass_utils import run_bass_kernel_spmd

F32 = mybir.dt.float32
BF16 = mybir.dt.bfloat16
I32 = mybir.dt.int32
AF = mybir.ActivationFunctionType
ALU = mybir.AluOpType
AX = mybir.AxisListType

D = 2048
S = 4096
NB = 4
TG = 512
NTG = 8
NSL = 4
OWN = [[0, 3, 4, 7], [1, 2, 5, 6]]
NEG = -30720.0
THETA = 500000.0
ALPHA = 2.0 ** 0.25
MLA_SCALE = 1.0 / math.sqrt(192.0)
MOBA_SCALE = 1.0 / math.sqrt(128.0)
LN_EPS = 1e-5
RMS_EPS = 1e-6
MAGIC = 12582912.0
TWO_PI_S = 6.283185
SB_BASE = 16512
SB_END = 229376 - 256

PC_QN = 0
PC_KVN = 3
PC_L1G = 5
PC_L1B = 21
PC_L2G = 37
PC_L2B = 53
PC_ROPE = 69
PC_N = 77
CB_ONES = 0
CB_ID = 128
CB_PM = 256
CB_PA = 384
CB_E = 512
CB_N = 512 + 2048

ENGS = ("sp", "pool", "pe", "act", "dve")


class Buf:
    __slots__ = ("name", "w", "r")

    def __init__(self, name=""):
        self.name = name
        self.w = []
        self.r = []


class Prog:
    def __init__(self, nc, stack):
        self.nc = nc
        self.stack = stack
        self.q = {e: [] for e in ENGS}
        self.esem = {}
        self.seen = {e: {} for e in ENGS}
        self.dma_pool = {"sp": [], "pool": []}
        self.dma_i = {"sp": 0, "pool": 0}
        self.nsem = 0
        self.NDMA = 12
        for e in ("pe", "act", "dve", "pool"):
            self._new_esem(e)
        for e in ("sp", "pool"):
            for i in range(self.NDMA):
                h = stack.enter_context(nc.semaphore(f"d_{e}_{i}"))
                self.dma_pool[e].append([h, 0, None])

    def _new_esem(self, e):
        self.nsem += 1
        h = self.stack.enter_context(self.nc.semaphore(f"e_{e}_{self.nsem}"))
        self.esem[e] = [h, 0]

    def _waits_for(self, eng, reads, writes, extra):
        toks = []
        for b in reads:
            toks.extend(b.w)
        for b in writes:
            toks.extend(b.w)
            toks.extend(b.r)
        toks.extend(extra)
        best = {}
        for (h, v, src) in toks:
            if src == "pe" and eng == "pe":
                continue
            k = id(h)
            if k not in best or best[k][1] < v:
                best[k] = (h, v)
        out = []
        seen = self.seen[eng]
        for k, (h, v) in best.items():
            if seen.get(k, -1) >= v:
                continue
            seen[k] = v
            out.append((h, v))
        return out

    def op(self, eng, fn, reads=(), writes=(), extra=(), signal=True):
        waits = self._waits_for(eng, reads, writes, extra)
        tok = None
        inc = None
        if signal:
            es = self.esem[eng]
            if es[1] >= 30000:
                self._new_esem(eng)
                es = self.esem[eng]
            es[1] += 1
            tok = (es[0], es[1], eng)
            inc = (es[0], 1)
        self.q[eng].append((waits, fn, inc))
        if tok is not None:
            for b in reads:
                b.r.append(tok)
            for b in writes:
                b.w = [tok]
                b.r = []
        return tok

    def dma(self, eng, fn, reads=(), writes=(), extra=(), commit=True):
        i = self.dma_i[eng] % self.NDMA
        self.dma_i[eng] += 1
        slot = self.dma_pool[eng][i]
        ex = list(extra)
        if slot[2] is not None:
            ex.append(slot[2])
        waits = self._waits_for(eng, reads, writes, ex)
        slot[1] += 16
        tok = (slot[0], slot[1], "dma")
        slot[2] = tok
        self.q[eng].append((waits, fn, (slot[0], 16)))
        if commit:
            for b in reads:
                b.r.append(tok)
            for b in writes:
                b.w = [tok]
                b.r = []
        return tok

    def dma_multi(self, eng, fns, reads=(), writes=()):
        toks = [self.dma(eng, fn, reads=reads, writes=writes, commit=False) for fn in fns]
        for b in reads:
            b.r.extend(toks)
        for b in writes:
            b.w = list(toks)
            b.r = []
        return toks

    def all_tokens(self):
        toks = []
        for e, (h, cnt) in self.esem.items():
            if cnt > 0:
                toks.append((h, cnt, "bar"))
        for e in ("sp", "pool"):
            for slot in self.dma_pool[e]:
                if slot[2] is not None:
                    toks.append((slot[2][0], slot[2][1], "bar"))
        return toks

    def barrier(self):
        toks = self.all_tokens()
        for eng in ENGS:
            waits = self._waits_for(eng, (), (), toks)
            self.q[eng].append((waits, None, None))

    def final_wait(self, eng):
        waits = self._waits_for(eng, (), (), self.all_tokens())
        self.q[eng].append((waits, None, None))

    def emit(self, block):
        def run(e, items):
            for waits, fn, inc in items:
                for (h, v) in waits:
                    e.wait_ge(h, v)
                if fn is None:
                    continue
                ins = fn(e)
                if inc is not None:
                    ins.then_inc(inc[0], inc[1])

        q = self.q

        @block.sync
        def _(e):
            run(e, q["sp"])

        @block.gpsimd
        def _(e):
            run(e, q["pool"])

        @block.tensor
        def _(e):
            run(e, q["pe"])

        @block.scalar
        def _(e):
            run(e, q["act"])

        @block.vector
        def _(e):
            run(e, q["dve"])


class Arena:
    def __init__(self, nc, base, end):
        self.nc = nc
        self.base = base
        self.end = end
        self.off = base
        self.n = 0

    def mark(self):
        return self.off

    def reset(self, mark):
        self.off = mark

    def alloc(self, name, shape, dt):
        nbytes = int(np.prod(shape[1:])) * mybir.dt.size(dt)
        nbytes = (nbytes + 31) // 32 * 32
        assert self.off + nbytes <= self.end, f"SBUF arena overflow at {name}: {self.off + nbytes - self.base}"
        self.n += 1
        t = self.nc.alloc_sbuf_tensor_at(f"{name}_{self.n}", list(shape), dt, offset=self.off)
        self.off += nbytes
        self.hw = max(getattr(self, 'hw', 0), self.off)
        return t


def f_mm(o, l, r, st, sp):
    return lambda e: e.matmul(o, l, r, start=st, stop=sp)


def f_act(o, i, func, scale=None, bias=None):
    kw = {}
    if scale is not None:
        kw["scale"] = scale
    if bias is not None:
        kw["bias"] = bias
    return lambda e: e.activation(out=o, in_=i, func=func, **kw)


def f_copy(o, i):
    return lambda e: e.tensor_copy(out=o, in_=i)


def f_tt(o, a, b, op):
    return lambda e: e.tensor_tensor(out=o, in0=a, in1=b, op=op)


def f_ts(o, a, s1, s2, op0, op1=None):
    if op1 is None:
        return lambda e: e.tensor_scalar(out=o, in0=a, scalar1=s1, scalar2=None, op0=op0)
    return lambda e: e.tensor_scalar(out=o, in0=a, scalar1=s1, scalar2=s2, op0=op0, op1=op1)


def f_stt(o, a, s, b, op0, op1):
    return lambda e: e.scalar_tensor_tensor(out=o, in0=a, scalar=s, in1=b, op0=op0, op1=op1)


def f_red(o, i, op):
    return lambda e: e.tensor_reduce(out=o, in_=i, axis=AX.X, op=op)


def f_recip(o, i):
    return lambda e: e.reciprocal(out=o, in_=i)


def f_memset(o, v):
    return lambda e: e.memset(o, v)


def f_dma(o, i):
    return lambda e: e.dma_start(out=o, in_=i)


def MM(P, outb, out_ap, items):
    n = len(items)
    allb = []
    tok = None
    for i, (l, r, bufs) in enumerate(items):
        for b in bufs:
            if b not in allb:
                allb.append(b)
        last = i == n - 1
        tok = P.op("pe", f_mm(out_ap, l, r, i == 0, last),
                   reads=(allb if last else bufs),
                   writes=([outb] if (i == 0 or last) else ()),
                   signal=last)
    return tok


def build_program(debug=False, phases=("A", "B", "C", "MLA", "MOBA", "POST")):
    nc = bass.Bass("TRN2", target_bir_lowering=False)
    dk = "ExternalOutput" if debug else "Internal"

    def din(name, shape, dt):
        return nc.dram_tensor(name, list(shape), dt, kind="ExternalInput").ap()

    def dscr(name, shape, dt):
        return nc.dram_tensor(name, list(shape), dt, kind=dk).ap()

    xT = din("xT", [D, S], F32)
    xTo = din("xTo", [D, 2048], F32)
    pos = din("pos", [1, S], I32)
    poso = din("poso", [1, 2048], I32)
    wA = din("wA", [D, 1408], F32)
    wB = din("wB", [D, 1024], F32)
    wC = din("wC", [D, 1408], F32)
    wuq_d = din("wuq", [384, 1536], F32)
    wukv_d = din("wukv", [256, 2048], F32)
    wout_t = din("wout_t", [16, 128, 16, 128], F32)
    wup_t = din("wup_t", [64, 128, 16, 128], F32)
    wdn_t = din("wdn_t", [8, 16, 128, 8, 128], F32)
    pcol_d = din("pcol", [128, PC_N], F32)
    cbf_d = din("cbf", [128, CB_N], BF16)
    cmask_d = din("cmask", [4, 8, 128, 512], BF16)
    gc_d = din("gc", [128, 3, 256], F32)
    yT = nc.dram_tensor("yT", [D, 2048], F32, kind="ExternalOutput").ap()

    KTm = dscr("KTm", [8, 128, S], BF16)
    Vm = dscr("Vm", [8, S, 128], BF16)
    QTm = dscr("QTm", [8, 128, 2048], BF16)
    KTa = dscr("KTa", [8, 128, S], BF16)
    KR = dscr("KR", [128, S], BF16)
    Va = dscr("Va", [8, S, 128], BF16)
    QTa = dscr("QTa", [8, 128, 2048], BF16)
    QRa = dscr("QRa", [4, 128, 2048], BF16)
    AT = dscr("AT", [16, 128, 2048], BF16)

    with ExitStack() as st:
        P = Prog(nc, st)
        A = Arena(nc, SB_BASE, SB_END)
        psall = st.enter_context(nc.psum_tensor("psall", [128, 4096], F32))
        ps = [psall[:, i * 512:(i + 1) * 512] for i in range(8)]
        psb = [Buf(f"ps{i}") for i in range(8)]

        pcol = A.alloc("pcol", [128, PC_N + 32], F32)
        b_pcol = Buf("pcol")
        cbf = A.alloc("cbf", [128, CB_N], BF16)
        b_cbf = Buf("cbf")
        kmean = A.alloc("kmean", [128, 128], BF16)
        b_kmean = Buf("kmean")
        ksum = A.alloc("ksum", [128, 128], F32)
        b_ksum = [[Buf() for _ in range(NTG)] for _ in range(8)]
        epsb = A.alloc("epsb", [128, 8], F32)
        b_eps = Buf()
        P.dma("sp", f_dma(pcol[:, 0:PC_N], pcol_d[:, :]), writes=[b_pcol])
        P.dma("sp", f_dma(cbf[:], cbf_d[:, :]), writes=[b_cbf])
        PC_L1GA = PC_N
        PC_L1BA = PC_N + 16
        P.op("dve", f_ts(pcol[:, PC_N:PC_N + 32], pcol[:, PC_L1G:PC_L1G + 32], ALPHA, None, ALU.mult),
             reads=[b_pcol], writes=[b_pcol])
        P.op("dve", f_memset(epsb[:, 0:1], RMS_EPS), writes=[b_eps])
        P.op("dve", f_memset(epsb[:, 1:2], LN_EPS), reads=[b_eps], writes=[b_eps])
        eps_rms = epsb[:, 0:1]
        eps_ln = epsb[:, 1:2]
        ones = cbf[:, CB_ONES:CB_ONES + 128]
        ident = cbf[:, CB_ID:CB_ID + 128]
        Pm = cbf[:, CB_PM:CB_PM + 128]
        Pa = cbf[:, CB_PA:CB_PA + 128]
        gmark = A.mark()

        rr = {"ev": 0, "bank": 0}

        def evac(out_ap, in_ap, reads, writes):
            rr["ev"] += 1
            if rr["ev"] % 2 == 0:
                return P.op("act", f_act(out_ap, in_ap, AF.Copy), reads=reads, writes=writes)
            return P.op("dve", f_copy(out_ap, in_ap), reads=reads, writes=writes)

        def nbank(lo, hi):
            rr["bank"] += 1
            return lo + rr["bank"] % (hi - lo)

        if any(p in phases for p in ("A", "B", "C")):
            Wr = [A.alloc("Wr0", [128, 16, 1408], BF16), A.alloc("Wr1", [128, 16, 1408], BF16)]
            b_Wr = [Buf("Wr0"), Buf("Wr1")]
            wuq = A.alloc("wuq", [128, 3, 1536], BF16)
            wukv = A.alloc("wukv", [128, 2, 2048], BF16)
            b_wuq = Buf("wuq")
            b_wukv = Buf("wukv")
            xb = [A.alloc("xb0", [128, 16, 512], BF16), A.alloc("xb1", [128, 16, 512], BF16)]
            b_xb = [Buf("xb0"), Buf("xb1")]
            posi = A.alloc("posi", [128, 512], I32)
            posf = A.alloc("posf", [128, 512], F32)
            b_posi = Buf()
            b_posf = Buf()
            turn = A.alloc("turn", [128, 512], F32)
            kk = A.alloc("kk", [128, 512], F32)
            b_turn = Buf()
            b_kk = Buf()
            tabs2 = [[A.alloc(f"tab{k}_{i}", [128, 512], F32) for i in range(4)] for k in range(2)]
            b_tabs2 = [[Buf() for _ in range(4)] for _ in range(2)]
            cur = {"set": 0}
            NHB = 3
            hb = [A.alloc(f"hb{i}", [128, 512], BF16) for i in range(NHB)]
            b_hb = [Buf() for _ in range(NHB)]
            t1 = [A.alloc(f"t1_{i}", [128, 512], F32) for i in range(2)]
            b_t1 = [Buf() for _ in range(2)]
            t2 = [A.alloc(f"t2_{i}", [128, 512], F32) for i in range(2)]
            b_t2 = [Buf() for _ in range(2)]
            n32 = [A.alloc(f"n32_{i}", [128, 512], F32) for i in range(2)]
            b_n32 = [Buf() for _ in range(2)]
            cb = [A.alloc(f"cb{i}", [128, 512], BF16) for i in range(3)]
            b_cb = [Buf() for _ in range(3)]
            sq = [A.alloc(f"sq{i}", [128, 512], BF16) for i in range(3)]
            b_sq = [Buf() for _ in range(3)]
            cn = [A.alloc(f"cn{i}", [128, 512], BF16) for i in range(3)]
            b_cn = [Buf() for _ in range(3)]
            rt = A.alloc("rt", [128, 512], F32)
            b_rt = Buf()
            rstd = A.alloc("rstd", [128, 512], F32)
            b_rstd = Buf()
            NKS = 4
            kst = [A.alloc(f"kst{i}", [128, 512], BF16) for i in range(NKS)]
            b_kst = [Buf() for _ in range(NKS)]
            vst = [A.alloc(f"vst{i}", [128, 1024], BF16) for i in range(2)]
            b_vst = [[Buf(), Buf()] for _ in range(2)]
            cnt = {"hb": 0, "t": 0, "kst": 0, "vst": 0, "xb": 0}

            def load_w(dst_i, src, ncols):
                v = src.rearrange("(c p) n -> p c n", p=128)
                cuts = [0, 384, 768, 1088, 1408] if ncols == 1408 else [0, 256, 512, 768, 1024]
                fns = []
                for g in range(4):
                    for cg in range(2):
                        fns.append(f_dma(Wr[dst_i][:, 8 * cg:8 * cg + 8, cuts[g]:cuts[g + 1]], v[:, 8 * cg:8 * cg + 8, cuts[g]:cuts[g + 1]]))
                P.dma_multi("pool", fns, writes=[b_Wr[dst_i]])

            def load_x(src, col0):
                i = cnt["xb"] % 2
                cnt["xb"] += 1
                v = src.rearrange("(c p) t -> p c t", p=128)
                P.dma_multi("pool", [f_dma(xb[i][:, 4 * g:4 * g + 4, :], v[:, 4 * g:4 * g + 4, col0:col0 + 512]) for g in range(4)],
                            writes=[b_xb[i]])
                return i

            def make_tables(psrc, col0, tset):
                tabs, b_tabs = tabs2[tset], b_tabs2[tset]
                P.dma("sp", f_dma(posi[:], psrc[0:1, col0:col0 + 512].partition_broadcast(128)), writes=[b_posi])
                P.op("dve", f_copy(posf[:], posi[:]), reads=[b_posi], writes=[b_posf])
                for ti in range(4):
                    ci = PC_ROPE + 2 * ti
                    P.op("dve", f_ts(turn[:], posf[:], pcol[:, ci:ci + 1], pcol[:, ci + 1:ci + 2], ALU.mult, ALU.add),
                         reads=[b_posf, b_pcol], writes=[b_turn])
                    P.op("dve", f_ts(kk[:], turn[:], MAGIC, None, ALU.add), reads=[b_turn], writes=[b_kk])
                    P.op("dve", f_ts(kk[:], kk[:], MAGIC, None, ALU.subtract), reads=[b_kk], writes=[b_kk])
                    P.op("dve", f_tt(kk[:], turn[:], kk[:], ALU.subtract), reads=[b_turn, b_kk], writes=[b_kk])
                    P.op("act", f_act(tabs[ti][:], kk[:], AF.Sin, scale=TWO_PI_S), reads=[b_kk], writes=[b_tabs[ti]])

            def rope_sum(ti, out_ap, out_bufs):
                P.op("pool", f_tt(out_ap, t1[ti][:], t2[ti][:], ALU.add), reads=[b_t1[ti], b_t2[ti]], writes=out_bufs)

            def next_kst():
                i = cnt["kst"] % NKS
                cnt["kst"] += 1
                return i

            def rms_part1(xi, wsrc, b_w, col0, nch):
                for j in range(nch):
                    bk = nbank(0, 5)
                    MM(P, psb[bk], ps[bk][:], [(wsrc[:, c, col0 + j * 128:col0 + (j + 1) * 128], xb[xi][:, c, :],
                                                [b_xb[xi], b_w]) for c in range(16)])
                    P.op("act", f_act(cb[j][:], ps[bk][:], AF.Copy), reads=[psb[bk]], writes=[b_cb[j]])
                    P.op("act", f_act(sq[j][:], ps[bk][:], AF.Square), reads=[psb[bk]], writes=[b_sq[j]])

            def rms_part2(nch, gcol, inv_n):
                sk = nbank(5, 8)
                MM(P, psb[sk], ps[sk][:], [(ones, sq[j][:], [b_sq[j], b_cbf]) for j in range(nch)])
                P.op("act", f_act(rt[:], ps[sk][:], AF.Sqrt, scale=inv_n, bias=eps_rms), reads=[psb[sk], b_eps], writes=[b_rt])
                P.op("dve", f_recip(rstd[:], rt[:]), reads=[b_rt], writes=[b_rstd])
                for j in range(nch):
                    P.op("dve", f_stt(cn[j][:], cb[j][:], pcol[:, gcol + j:gcol + j + 1], rstd[:], ALU.mult, ALU.mult),
                         reads=[b_cb[j], b_rstd, b_pcol], writes=[b_cn[j]])

            def rope_part1(bank, typ):
                hi = cnt["hb"] % NHB
                cnt["hb"] += 1
                P.op("act", f_act(hb[hi][:], ps[bank][:], AF.Copy), reads=[psb[bank]], writes=[b_hb[hi]])
                return (hi, typ)

            def rope_part2(state):
                hi, typ = state
                tabs, b_tabs = tabs2[cur["set"]], b_tabs2[cur["set"]]
                Pmat = Pm if typ == 0 else Pa
                Ct, Sg = tabs[2 * typ], tabs[2 * typ + 1]
                bC, bS = b_tabs[2 * typ], b_tabs[2 * typ + 1]
                ti = cnt["t"] % 2
                cnt["t"] += 1
                xbk = nbank(5, 8)
                MM(P, psb[xbk], ps[xbk][:], [(Pmat, hb[hi][:], [b_hb[hi], b_cbf])])
                P.op("pool", f_tt(t1[ti][:], hb[hi][:], Ct[:], ALU.mult), reads=[b_hb[hi], bC], writes=[b_t1[ti]])
                P.op("dve", f_tt(t2[ti][:], ps[xbk][:], Sg[:], ALU.mult), reads=[psb[xbk], bS], writes=[b_t2[ti]])
                return ti

            P.dma("pool", f_dma(wuq[:], wuq_d.rearrange("(c p) n -> p c n", p=128)), writes=[b_wuq])
            P.dma("pool", f_dma(wukv[:], wukv_d.rearrange("(c p) n -> p c n", p=128)), writes=[b_wukv])

            nxt = None
            if "A" in phases:
                nxt = load_x(xT, 0)
                load_w(0, wA, 1408)
                WA = Wr[0]
                for tg in range(NTG):
                    xi = nxt
                    if tg + 1 < NTG:
                        nxt = load_x(xT, (tg + 1) * 512)
                    elif "B" in phases:
                        load_w(1, wB, 1024)
                        nxt = load_x(xT, 0)
                    if tg == 0:
                        make_tables(pos, 0, 0)
                    cur["set"] = tg % 2
                    c0 = tg * 512
                    pend = []

                    def up_k(h, c0=c0):
                        bk = nbank(0, 5)
                        MM(P, psb[bk], ps[bk][:], [(wukv[:, r, h * 128:(h + 1) * 128], cn[r][:], [b_cn[r], b_wukv]) for r in range(2)])
                        ki = next_kst()
                        evac(kst[ki][:], ps[bk][:], [psb[bk]], [b_kst[ki]])
                        P.dma("sp", f_dma(KTa[h, :, c0:c0 + 512], kst[ki][:]), reads=[b_kst[ki]])

                    def up_v(tt, c0=c0):
                        vi = cnt["vst"] % 2
                        cnt["vst"] += 1
                        for half in range(2):
                            bk = nbank(0, 5)
                            MM(P, psb[bk], ps[bk][:], [(cn[r][:, tt * 128:(tt + 1) * 128],
                                                        wukv[:, r, 1024 + half * 512:1024 + (half + 1) * 512],
                                                        [b_cn[r], b_wukv]) for r in range(2)])
                            evac(vst[vi][:, half * 512:(half + 1) * 512], ps[bk][:], [psb[bk]], [b_vst[vi][half]])
                        r0 = c0 + tt * 128
                        P.dma("sp", f_dma(Va[:, r0:r0 + 128, :].rearrange("h p d -> p h d"),
                                          vst[vi][:].rearrange("p (h d) -> p h d", h=8)), reads=b_vst[vi])

                    def fin_kr(state, c0=c0):
                        ti = rope_part2(state)
                        ki = next_kst()
                        rope_sum(ti, kst[ki][:], [b_kst[ki]])
                        P.dma("sp", f_dma(KR[:, c0:c0 + 512], kst[ki][:]), reads=[b_kst[ki]])

                    def fin_mk(state, h, tg=tg, c0=c0):
                        ti = rope_part2(state)
                        rope_sum(ti, n32[ti][:], [b_n32[ti]])
                        P.op("dve", f_red(ksum[:, h * 16 + 2 * tg:h * 16 + 2 * tg + 2],
                                          n32[ti][:].rearrange("p (a b) -> p a b", a=2), ALU.add),
                             reads=[b_n32[ti]], writes=[b_ksum[h][tg]])
                        ki = next_kst()
                        P.op("act", f_act(kst[ki][:], n32[ti][:], AF.Copy), reads=[b_n32[ti]], writes=[b_kst[ki]])
                        P.dma("sp", f_dma(KTm[h, :, c0:c0 + 512], kst[ki][:]), reads=[b_kst[ki]])

                    rms_part1(xi, WA, b_Wr[0], 0, 2)
                    bk = nbank(0, 5)
                    MM(P, psb[bk], ps[bk][:], [(WA[:, c, 256:384], xb[xi][:, c, :], [b_xb[xi], b_Wr[0]]) for c in range(16)])
                    st_kr = rope_part1(bk, 1)
                    rms_part2(2, PC_KVN, 1.0 / 256.0)
                    ups = [(lambda h=h: up_k(h)) for h in range(8)] + [(lambda tt=tt: up_v(tt)) for tt in range(4)]
                    prev = lambda: fin_kr(st_kr)
                    for h in range(8):
                        bk = nbank(0, 5)
                        MM(P, psb[bk], ps[bk][:], [(WA[:, c, 384 + h * 128:384 + (h + 1) * 128], xb[xi][:, c, :],
                                                    [b_xb[xi], b_Wr[0]]) for c in range(16)])
                        st_h = rope_part1(bk, 0)
                        prev()
                        prev = (lambda s_=st_h, h=h: fin_mk(s_, h))
                        if h >= 1:
                            for _ in range(2):
                                if ups:
                                    ups.pop(0)()
                        if h == 2 and tg + 1 < NTG:
                            make_tables(pos, (tg + 1) * 512, (tg + 1) % 2)
                    prev()
                    while ups:
                        ups.pop(0)()
                allks = [b for hh in b_ksum for b in hh]
                P.op("dve", f_ts(kmean[:], ksum[:], 1.0 / 256.0, None, ALU.mult), reads=allks, writes=[b_kmean])

            if "B" in phases:
                if "A" not in phases:
                    load_w(1, wB, 1024)
                    nxt = load_x(xT, 0)
                WB = Wr[1]
                if "C" in phases:
                    make_tables(poso, 0, 0)
                for tg in range(NTG):
                    xi = nxt
                    if tg + 1 < NTG:
                        nxt = load_x(xT, (tg + 1) * 512)
                    elif "C" in phases:
                        load_w(0, wC, 1408)
                        nxt = load_x(xTo, 0)
                    c0 = tg * 512
                    for tt in range(4):
                        vi = cnt["vst"] % 2
                        cnt["vst"] += 1
                        for half in range(2):
                            bk = nbank(0, 8)
                            MM(P, psb[bk], ps[bk][:], [(xb[xi][:, c, tt * 128:(tt + 1) * 128],
                                                        WB[:, c, half * 512:(half + 1) * 512],
                                                        [b_xb[xi], b_Wr[1]]) for c in range(16)])
                            evac(vst[vi][:, half * 512:(half + 1) * 512], ps[bk][:], [psb[bk]], [b_vst[vi][half]])
                        r0 = c0 + tt * 128
                        P.dma("sp", f_dma(Vm[:, r0:r0 + 128, :].rearrange("h p d -> p h d"),
                                          vst[vi][:].rearrange("p (h d) -> p h d", h=8)), reads=b_vst[vi])

            if "C" in phases:
                if "B" not in phases:
                    load_w(0, wC, 1408)
                    nxt = load_x(xTo, 0)
                WC = Wr[0]
                for s in range(NSL):
                    xi = nxt
                    if s + 1 < NSL:
                        nxt = load_x(xTo, (s + 1) * 512)
                    if s == 0 and "B" not in phases:
                        make_tables(poso, 0, 0)
                    cur["set"] = s % 2
                    c0 = s * 512
                    def up_qn(h, c0=c0):
                        bk = nbank(0, 5)
                        MM(P, psb[bk], ps[bk][:], [(wuq[:, r, h * 128:(h + 1) * 128], cn[r][:], [b_cn[r], b_wuq]) for r in range(3)])
                        ki = next_kst()
                        evac(kst[ki][:], ps[bk][:], [psb[bk]], [b_kst[ki]])
                        P.dma("sp", f_dma(QTa[h, :, c0:c0 + 512], kst[ki][:]), reads=[b_kst[ki]])

                    def up_qr1(pr):
                        bk = nbank(0, 5)
                        MM(P, psb[bk], ps[bk][:], [(wuq[:, r, 1024 + pr * 128:1024 + (pr + 1) * 128], cn[r][:], [b_cn[r], b_wuq])
                                                    for r in range(3)])
                        return rope_part1(bk, 1)

                    def fin_rope(state, dst):
                        ti = rope_part2(state)
                        ki = next_kst()
                        rope_sum(ti, kst[ki][:], [b_kst[ki]])
                        P.dma("sp", f_dma(dst, kst[ki][:]), reads=[b_kst[ki]])

                    rms_part1(xi, WC, b_Wr[0], 0, 3)
                    prev = None
                    qr_states = []
                    for h in range(8):
                        bk = nbank(0, 5)
                        MM(P, psb[bk], ps[bk][:], [(WC[:, c, 384 + h * 128:384 + (h + 1) * 128], xb[xi][:, c, :],
                                                    [b_xb[xi], b_Wr[0]]) for c in range(16)])
                        st_h = rope_part1(bk, 0)
                        if h == 0:
                            rms_part2(3, PC_QN, 1.0 / 384.0)
                        if prev is not None:
                            prev()
                        prev = (lambda s_=st_h, h=h: fin_rope(s_, QTm[h, :, c0:c0 + 512]))
                        if h >= 1:
                            up_qn(h - 1)
                        if h == 2 and s + 1 < NSL:
                            make_tables(poso, (s + 1) * 512, (s + 1) % 2)
                        if h >= 2 and h - 2 < 4:
                            pr = h - 2
                            if qr_states:
                                pr0, s0 = qr_states.pop(0)
                                fin_rope(s0, QRa[pr0, :, c0:c0 + 512])
                            qr_states.append((pr, up_qr1(pr)))
                    prev()
                    up_qn(7)
                    while qr_states:
                        pr0, s0 = qr_states.pop(0)
                        fin_rope(s0, QRa[pr0, :, c0:c0 + 512])
            P.barrier()
            A.reset(gmark)

        if "MLA" in phases or "MOBA" in phases:
            cmask = A.alloc("cmask", [128, 32, 512], BF16)
            b_cmask = Buf()
            cmv = cmask_d.rearrange("s i p q -> p (s i) q")
            P.dma_multi("sp", [f_dma(cmask[:, 8 * g:8 * g + 8, :], cmv[:, 8 * g:8 * g + 8, :]) for g in range(4)], writes=[b_cmask])
            KRe = A.alloc("KRe", [128, S], BF16)
            KRo = A.alloc("KRo", [128, S], BF16)
            b_KRe, b_KRo = Buf(), Buf()
            P.op("dve", f_memset(KRe[64:128, :], 0.0), writes=[b_KRe])
            P.op("dve", f_memset(KRo[0:64, :], 0.0), writes=[b_KRo])
            P.dma("sp", f_dma(KRe[0:64, :], KR[0:64, :]), reads=[b_KRe], writes=[b_KRe])
            P.dma("sp", f_dma(KRo[64:128, :], KR[64:128, :]), reads=[b_KRo], writes=[b_KRo])
            zf = A.alloc("zf", [128, 1024], F32)
            b_zf = Buf()
            P.op("dve", f_memset(zf[:], 0.0), writes=[b_zf])
            gcs = A.alloc("gcs", [128, 3, 256], F32)
            b_gcs = Buf()
            P.dma("sp", f_dma(gcs[:], gc_d[:, :, :]), writes=[b_gcs])
            Kh = [A.alloc(f"Kh{i}", [128, S], BF16) for i in range(2)]
            Vh = [A.alloc(f"Vh{i}", [128, 32, 128], BF16) for i in range(2)]
            Qh = [A.alloc(f"Qh{i}", [128, 2048], BF16) for i in range(2)]
            QRh = [A.alloc(f"QRh{i}", [128, 2048], BF16) for i in range(2)]
            b_Kh = [Buf(), Buf()]
            b_Vh = [Buf(), Buf()]
            b_Qh = [Buf(), Buf()]
            b_QRh = [Buf(), Buf()]
            NPT = 6
            PT = [A.alloc(f"PT{i}", [128, 1024], BF16) for i in range(NPT)]
            b_PT = [Buf() for _ in range(NPT)]
            negT = [A.alloc(f"negT{i}", [128, 512], BF16) for i in range(2)]
            b_negT = [Buf(), Buf()]
            for i in range(2):
                P.op("dve", f_memset(negT[i][:], 0.0), writes=[b_negT[i]])
            gm = [A.alloc(f"gm{i}", [128, 256], F32) for i in range(4)]
            b_gm = [Buf() for _ in range(4)]
            mx = A.alloc("mx", [128, 16], F32)
            b_mx = Buf()
            negm = [A.alloc(f"negm{i}", [128, 256], BF16) for i in range(2)]
            b_negm = [Buf(), Buf()]
            rinv = A.alloc("rinv", [128, 512], F32)
            b_rinv = Buf()
            accf = [[A.alloc(f"accf{i}{j}", [128, 1024], F32) for j in range(2)] for i in range(2)]
            b_accf = [[Buf(), Buf()] for _ in range(2)]
            accb = [A.alloc(f"accb{j}", [128, 1024], BF16) for j in range(2)]
            b_accb = [Buf(), Buf()]
            ost = [A.alloc(f"ost{i}", [128, 512], BF16) for i in range(2)]
            b_ost = [Buf(), Buf()]
            c2 = {"pt": 0, "sp": 0, "ol": 0, "ost": 0, "negT": 0}
            SP_ = [(0, 1), (2, 3)]
            OB = [4, 5]
            LBK = 6
            GB = 7

            heads = []
            if "MLA" in phases:
                heads += [("a", h) for h in range(8)]
            if "MOBA" in phases:
                heads += [("m", h) for h in range(8)]

            def load_head(idx):
                typ, h = heads[idx]
                hs = idx % 2
                Ksrc, Vsrc, Qsrc = (KTa, Va, QTa) if typ == "a" else (KTm, Vm, QTm)
                P.dma("sp", f_dma(Qh[hs][:], Qsrc[h, :, :]), writes=[b_Qh[hs]])
                P.dma("sp", f_dma(Kh[hs][:], Ksrc[h, :, :]), writes=[b_Kh[hs]])
                P.dma_multi("sp", [f_dma(Vh[hs][:, q4 * 8:(q4 + 1) * 8, :],
                                         Vsrc[h, q4 * 1024:(q4 + 1) * 1024, :].rearrange("(t p) d -> p t d", p=128))
                                   for q4 in range(4)], writes=[b_Vh[hs]])
                if typ == "a" and h % 2 == 0:
                    pi = (h // 2) % 2
                    P.dma("sp", f_dma(QRh[pi][:], QRa[h // 2, :, :]), writes=[b_QRh[pi]])

            def gating(idx):
                typ, h = heads[idx]
                hs = idx % 2
                for t in range(16):
                    P.op("pe", f_mm(ps[GB][:, t * 16:(t + 1) * 16], Qh[hs][:, t * 128:(t + 1) * 128],
                                    kmean[:, h * 16:(h + 1) * 16], True, True),
                         reads=[b_Qh[hs], b_kmean], writes=[psb[GB]], signal=(t == 15))
                g0, g1, g2, g3 = gm

                def v3(ap):
                    return ap.rearrange("p (t n) -> p t n", t=16)

                def mxb():
                    return mx[:].unsqueeze(2).to_broadcast([128, 16, 16])

                P.op("dve", f_tt(g0[:], ps[GB][:, 0:256], gcs[:, 0, :], ALU.add), reads=[psb[GB], b_gcs], writes=[b_gm[0]])
                src_, bsrc = g0, b_gm[0]
                for it in range(2):
                    P.op("dve", f_red(mx[:], v3(src_[:]), ALU.max), reads=[bsrc], writes=[b_mx])
                    P.op("dve", f_tt(v3(g3[:]), v3(src_[:]), mxb(), ALU.is_equal), reads=[bsrc, b_mx], writes=[b_gm[3]])
                    dst, bdst = (g1, b_gm[1]) if it == 0 else (g2, b_gm[2])
                    P.op("dve", f_stt(dst[:], g3[:], -1e30, src_[:], ALU.mult, ALU.add), reads=[b_gm[3], bsrc], writes=[bdst])
                    src_, bsrc = dst, bdst
                P.op("dve", f_red(mx[:], v3(g2[:]), ALU.max), reads=[b_gm[2]], writes=[b_mx])
                P.op("dve", f_tt(v3(g3[:]), v3(g0[:]), mxb(), ALU.is_ge), reads=[b_gm[0], b_mx], writes=[b_gm[3]])
                P.op("dve", f_tt(g3[:], g3[:], gcs[:, 1, :], ALU.mult), reads=[b_gm[3], b_gcs], writes=[b_gm[3]])
                P.op("dve", f_tt(g3[:], g3[:], gcs[:, 2, :], ALU.add), reads=[b_gm[3], b_gcs], writes=[b_gm[3]])
                P.op("dve", f_ts(negm[hs][:], g3[:], -NEG, NEG, ALU.mult, ALU.add), reads=[b_gm[3]], writes=[b_negm[hs]])

            pend_fin = []

            if heads:
                load_head(0)
                if heads[0][0] == "m":
                    gating(0)
            for idx, (typ, h) in enumerate(heads):
                hs = idx % 2
                if idx + 1 < len(heads):
                    load_head(idx + 1)
                scale = MLA_SCALE if typ == "a" else MOBA_SCALE
                for s in range(NSL):
                    nk = 8 * (s + 1)
                    npair = nk // 2
                    q0 = s * 512
                    ni = None
                    if typ == "m":
                        ni = c2["negT"] % 2
                        c2["negT"] += 1
                        for i4 in range(4):
                            t = 4 * s + i4
                            P.op("pe", f_mm(ps[GB][0:16, i4 * 128:(i4 + 1) * 128], negm[hs][:, t * 16:(t + 1) * 16], ident, True, True),
                                 reads=[b_negm[hs], b_cbf], writes=[psb[GB]], signal=(i4 == 3))
                        P.op("act", f_act(negT[ni][0:16, :], ps[GB][0:16, :], AF.Copy), reads=[psb[GB]], writes=[b_negT[ni]])
                    oi = c2["ol"] % 2
                    c2["ol"] += 1
                    ob = OB[oi]

                    def emit_qk_pair(pp):
                        b0, b1 = SP_[c2["sp"] % 2]
                        c2["sp"] += 1
                        for j, bk in enumerate((b0, b1)):
                            kt = 2 * pp + j
                            items = [(Kh[hs][:, kt * 128:(kt + 1) * 128], Qh[hs][:, q0:q0 + 512], [b_Kh[hs], b_Qh[hs]])]
                            if typ == "a":
                                pi = (h // 2) % 2
                                KRx, b_KRx = (KRe, b_KRe) if h % 2 == 0 else (KRo, b_KRo)
                                items.append((KRx[:, kt * 128:(kt + 1) * 128], QRh[pi][:, q0:q0 + 512], [b_KRx, b_QRh[pi]]))
                            else:
                                n = kt // 2
                                items.append((cbf[:, CB_E + n * 128:CB_E + (n + 1) * 128], negT[ni][:], [b_cbf, b_negT[ni]]))
                            if kt >= nk - 8:
                                items.append((ident, cmask[:, s * 8 + kt - (nk - 8), :], [b_cbf, b_cmask]))
                            MM(P, psb[bk], ps[bk][:], items)
                        return (b0, b1)

                    banks = [emit_qk_pair(0), emit_qk_pair(1)]
                    for pp in range(npair):
                        b0, b1 = banks[pp]
                        pi_ = c2["pt"] % NPT
                        c2["pt"] += 1
                        P.op("act", f_act(PT[pi_][:], psall[:, b0 * 512:b0 * 512 + 1024], AF.Exp, scale=scale),
                             reads=[psb[b0], psb[b1]], writes=[b_PT[pi_]])
                        for j in range(2):
                            kt = 2 * pp + j
                            first, last = kt == 0, kt == nk - 1
                            P.op("pe", f_mm(ps[ob][:], Vh[hs][:, kt, :], PT[pi_][:, j * 512:(j + 1) * 512], first, last),
                                 reads=[b_PT[pi_], b_Vh[hs]], writes=([psb[ob]] if (first or last) else []), signal=(j == 1))
                        ai = pp % 2
                        aeng = "pool" if ai == 0 else "dve"
                        if pp < 2:
                            P.op(aeng, f_tt(accf[oi][ai][:], zf[:], PT[pi_][:], ALU.add), reads=[b_PT[pi_], b_zf], writes=[b_accf[oi][ai]])
                        else:
                            P.op(aeng, f_tt(accf[oi][ai][:], accf[oi][ai][:], PT[pi_][:], ALU.add),
                                 reads=[b_PT[pi_], b_accf[oi][ai]], writes=[b_accf[oi][ai]])
                        if pp + 2 < npair:
                            banks.append(emit_qk_pair(pp + 2))
                        if pp == 1 and pend_fin:
                            pend_fin.pop(0)()
                        if pp == 1 and s == 1 and idx + 1 < len(heads) and heads[idx + 1][0] == "m":
                            gating(idx + 1)

                    def finalize(oi=oi, ob=ob, q0=q0, ch=(h if typ == "a" else 8 + h)):
                        for j in range(2):
                            P.op("act", f_act(accb[j][:], accf[oi][j][:], AF.Copy), reads=[b_accf[oi][j]], writes=[b_accb[j]])
                        MM(P, psb[LBK], ps[LBK][:], [(ones, accb[j][:, hh * 512:(hh + 1) * 512], [b_accb[j], b_cbf])
                                                      for j in range(2) for hh in range(2)])
                        P.op("act", f_act(rinv[:], ps[LBK][:], AF.Ln), reads=[psb[LBK]], writes=[b_rinv])
                        P.op("act", f_act(rinv[:], rinv[:], AF.Exp, scale=-1.0), reads=[b_rinv], writes=[b_rinv])
                        oi2 = c2["ost"] % 2
                        c2["ost"] += 1
                        P.op("dve", f_tt(ost[oi2][:], ps[ob][:], rinv[:], ALU.mult), reads=[psb[ob], b_rinv], writes=[b_ost[oi2]])
                        P.dma("sp", f_dma(AT[ch, :, q0:q0 + 512], ost[oi2][:]), reads=[b_ost[oi2]])

                    pend_fin.append(finalize)
            while pend_fin:
                pend_fin.pop(0)()
            P.barrier()
            A.reset(gmark)

        if "POST" in phases:
            R = A.alloc("R", [128, 16, 1024], F32)
            b_R = [[Buf(), Buf()] for _ in range(16)]
            x1T = A.alloc("x1T", [128, 16, 1024], BF16)
            b_x1 = [[Buf(), Buf()] for _ in range(16)]
            asl = A.alloc("asl", [128, 16, 512], BF16)
            b_asl = [Buf() for _ in range(16)]
            hid = A.alloc("hid", [128, 8, 1024], BF16)
            b_hid = [[Buf(), Buf()] for _ in range(8)]
            hid_ln = hid[:].rearrange("p k (a t) -> p (k a) t", a=2)
            NW = 4
            wt = [A.alloc(f"wt{i}", [128, 16, 128], BF16) for i in range(NW)]
            b_wt = [Buf() for _ in range(NW)]
            NWD = 6
            wd = [A.alloc(f"wd{i}", [128, 8, 128], BF16) for i in range(NWD)]
            b_wd = [Buf() for _ in range(NWD)]
            rl = [A.alloc(f"rl{i}", [128, 512], F32) for i in range(2)]
            b_rl = [Buf(), Buf()]
            mean = A.alloc("mean", [128, 512], F32)
            msq = A.alloc("msq", [128, 512], F32)
            var = A.alloc("var", [128, 512], F32)
            rs2 = A.alloc("rs2", [128, 512], F32)
            nmr = A.alloc("nmr", [128, 512], F32)
            b_mean, b_msq, b_var, b_rs2, b_nmr = Buf(), Buf(), Buf(), Buf(), Buf()
            tA = [A.alloc(f"tA{i}", [128, 512], F32) for i in range(2)]
            b_tA = [Buf(), Buf()]
            tB = [A.alloc(f"tB{i}", [128, 512], F32) for i in range(2)]
            b_tB = [Buf(), Buf()]
            c3 = {"wt": 0, "wd": 0, "rl": 0, "t": 0, "bank": 0}

            def pbank():
                c3["bank"] += 1
                return c3["bank"] % 6

            def layer_norm(half, gcol, bcol, first):
                hs_ = slice(half * 512, (half + 1) * 512)
                for c in range(16):
                    P.op("act", f_act(asl[:, c, :], R[:, c, hs_], AF.Copy), reads=[b_R[c][half]], writes=[b_asl[c]])
                    P.op("dve", f_tt(hid_ln[:, c, :], R[:, c, hs_], R[:, c, hs_], ALU.mult), reads=[b_R[c][half]], writes=[b_hid[c // 2][c % 2]])
                MM(P, psb[6], ps[6][:], [(ones, asl[:, c, :], [b_asl[c], b_cbf]) for c in range(16)])
                MM(P, psb[7], ps[7][:], [(ones, hid_ln[:, c, :], [b_hid[c // 2][c % 2], b_cbf]) for c in range(16)])
                P.op("dve", f_ts(mean[:], ps[6][:], 1.0 / D, None, ALU.mult), reads=[psb[6]], writes=[b_mean])
                P.op("dve", f_tt(msq[:], mean[:], mean[:], ALU.mult), reads=[b_mean], writes=[b_msq])
                P.op("dve", f_stt(var[:], ps[7][:], 1.0 / D, msq[:], ALU.mult, ALU.subtract), reads=[psb[7], b_msq], writes=[b_var])
                P.op("act", f_act(var[:], var[:], AF.Sqrt, scale=1.0, bias=eps_ln), reads=[b_var, b_eps], writes=[b_var])
                P.op("dve", f_recip(rs2[:], var[:]), reads=[b_var], writes=[b_rs2])
                P.op("dve", f_stt(nmr[:], mean[:], -1.0, rs2[:], ALU.mult, ALU.mult), reads=[b_mean, b_rs2], writes=[b_nmr])
                for c in range(16):
                    i = c3["t"] % 2
                    c3["t"] += 1
                    P.op("dve", f_tt(tA[i][:], R[:, c, hs_], rs2[:], ALU.mult), reads=[b_R[c][half], b_rs2], writes=[b_tA[i]])
                    P.op("dve", f_tt(tB[i][:], tA[i][:], nmr[:], ALU.add), reads=[b_tA[i], b_nmr], writes=[b_tB[i]])
                    if first:
                        P.op("act", f_act(x1T[:, c, hs_], tB[i][:], AF.Identity, scale=pcol[:, gcol + c:gcol + c + 1],
                                          bias=pcol[:, bcol + c:bcol + c + 1]), reads=[b_tB[i], b_pcol], writes=[b_x1[c][half]])
                        P.op("act", f_act(R[:, c, hs_], tB[i][:], AF.Identity, scale=pcol[:, PC_L1GA + c:PC_L1GA + c + 1],
                                          bias=pcol[:, PC_L1BA + c:PC_L1BA + c + 1]), reads=[b_tB[i], b_pcol], writes=[b_R[c][half]])
                    else:
                        P.op("act", f_act(R[:, c, hs_], tB[i][:], AF.Identity, scale=pcol[:, gcol + c:gcol + c + 1],
                                          bias=pcol[:, bcol + c:bcol + c + 1]), reads=[b_tB[i], b_pcol], writes=[b_R[c][half]])

            xov = xTo.rearrange("(c p) t -> p c t", p=128)
            yv = yT.rearrange("(c p) t -> p c t", p=128)
            atv = AT.rearrange("c p t -> p c t")
            for bl in range(2):
                asrc = [asl, hid_ln]
                b_asrc = [b_asl, [b_hid[c // 2][c % 2] for c in range(16)]]
                for half in range(2):
                    s = 2 * bl + half
                    q0 = s * 512
                    hs_ = slice(half * 512, (half + 1) * 512)
                    P.dma_multi("sp", [f_dma(asrc[half][:, 4 * g:4 * g + 4, :], atv[:, 4 * g:4 * g + 4, q0:q0 + 512]) for g in range(4)],
                                writes=b_asrc[half])
                    P.dma_multi("sp", [f_dma(R[:, 4 * g:4 * g + 4, hs_], xov[:, 4 * g:4 * g + 4, q0:q0 + 512]) for g in range(4)],
                                writes=[b_R[c][half] for c in range(16)])
                for dc in range(16):
                    wi = c3["wt"] % NW
                    c3["wt"] += 1
                    P.dma("pool", f_dma(wt[wi][:], wout_t[dc, :, :, :]), writes=[b_wt[wi]])
                    for half in range(2):
                        hs_ = slice(half * 512, (half + 1) * 512)
                        bk = pbank()
                        MM(P, psb[bk], ps[bk][:], [(wt[wi][:, c, :], asrc[half][:, c, :], [b_wt[wi], b_asrc[half][c]]) for c in range(16)])
                        P.op("dve", f_stt(R[:, dc, hs_], R[:, dc, hs_], ALPHA, ps[bk][:], ALU.mult, ALU.add),
                             reads=[psb[bk], b_R[dc][half]], writes=[b_R[dc][half]])
                for half in range(2):
                    layer_norm(half, PC_L1G, PC_L1B, True)
                for fb in range(8):
                    for fc in range(8):
                        wi = c3["wt"] % NW
                        c3["wt"] += 1
                        P.dma("pool", f_dma(wt[wi][:], wup_t[fb * 8 + fc, :, :, :]), writes=[b_wt[wi]])
                        for half in range(2):
                            hs_ = slice(half * 512, (half + 1) * 512)
                            bk = pbank()
                            MM(P, psb[bk], ps[bk][:], [(wt[wi][:, c, :], x1T[:, c, hs_], [b_wt[wi], b_x1[c][half]]) for c in range(16)])
                            ri = c3["rl"] % 2
                            c3["rl"] += 1
                            P.op("act", f_act(rl[ri][:], ps[bk][:], AF.Relu), reads=[psb[bk]], writes=[b_rl[ri]])
                            P.op("dve", f_tt(hid[:, fc, hs_], rl[ri][:], rl[ri][:], ALU.mult), reads=[b_rl[ri]], writes=[b_hid[fc][half]])
                    for dc in range(16):
                        wi = c3["wd"] % NWD
                        c3["wd"] += 1
                        P.dma("pool", f_dma(wd[wi][:], wdn_t[fb, dc, :, :, :]), writes=[b_wd[wi]])
                        for half in range(2):
                            hs_ = slice(half * 512, (half + 1) * 512)
                            bk = pbank()
                            MM(P, psb[bk], ps[bk][:], [(wd[wi][:, k, :], hid[:, k, hs_], [b_wd[wi], b_hid[k][half]]) for k in range(8)])
                            P.op("dve", f_tt(R[:, dc, hs_], R[:, dc, hs_], ps[bk][:], ALU.add),
                                 reads=[psb[bk], b_R[dc][half]], writes=[b_R[dc][half]])
                for half in range(2):
                    s = 2 * bl + half
                    q0 = s * 512
                    hs_ = slice(half * 512, (half + 1) * 512)
                    layer_norm(half, PC_L2G, PC_L2B, False)
                    P.dma_multi("sp", [f_dma(yv[:, 4 * g:4 * g + 4, q0:q0 + 512], R[:, 4 * g:4 * g + 4, hs_]) for g in range(4)],
                                reads=[b_R[c][half] for c in range(16)])

        for eng in ENGS:
            P.final_wait(eng)
        with nc.Block() as block:
            P.emit(block)
    return nc


def _rope_cols():
    cols = np.zeros((128, 8), np.float32)
    f16 = (THETA ** (-(np.arange(16, dtype=np.float32)) * np.float32(2.0 / 32))).astype(np.float32)
    inv_m = np.zeros(128, np.float32)
    inv_m[0:16] = f16
    inv_m[16:32] = f16
    inv_m = (inv_m.astype(np.float64) / (2 * math.pi)).astype(np.float32)
    cols[:, 0] = inv_m
    cols[:, 1] = 0.25
    cols[:, 2] = inv_m
    offs = np.zeros(128, np.float32)
    offs[0:16] = 0.5
    cols[:, 3] = offs
    f32_ = (THETA ** (-(np.arange(32, dtype=np.float32)) * np.float32(2.0 / 64))).astype(np.float32)
    inv_a = np.concatenate([f32_, f32_, f32_, f32_]).astype(np.float64) / (2 * math.pi)
    cols[:, 4] = inv_a.astype(np.float32)
    cols[:, 5] = 0.25
    cols[:, 6] = inv_a.astype(np.float32)
    offa = np.zeros(128, np.float32)
    offa[0:32] = 0.5
    offa[64:96] = 0.5
    cols[:, 7] = offa
    return cols


def _const_bf16():
    c = np.zeros((128, CB_N), np.float32)
    c[:, CB_ONES:CB_ONES + 128] = 1.0
    c[:, CB_ID:CB_ID + 128] = np.eye(128, dtype=np.float32)
    pm = np.zeros((128, 128), np.float32)
    for m in range(32):
        src = m + 16 if m < 16 else m - 16
        pm[src, m] = 1.0
    c[:, CB_PM:CB_PM + 128] = pm
    pa = np.zeros((128, 128), np.float32)
    for m in range(128):
        src = m + 32 if (m % 64) < 32 else m - 32
        pa[src, m] = 1.0
    c[:, CB_PA:CB_PA + 128] = pa
    for n in range(16):
        c[n, CB_E + n * 128:CB_E + (n + 1) * 128] = 1.0
    return c.astype(ml_dtypes.bfloat16)


def _core_consts(j):
    cm = np.zeros((4, 8, 128, 512), np.float32)
    gc = np.zeros((128, 3, 256), np.float32)
    for s in range(4):
        G = OWN[j][s]
        nk = 8 * (s + 1)
        qpos = G * 512 + np.arange(512)
        for i in range(8):
            kt = nk - 8 + i
            kpos = kt * 128 + np.arange(128)
            cm[s, i] = np.where(kpos[:, None] <= qpos[None, :], 0.0, NEG)
        for i4 in range(4):
            t = 4 * s + i4
            gt = G * 4 + i4
            qblk = gt // 2
            n = np.arange(16)
            gc[:, 0, t * 16:(t + 1) * 16] = np.where(n < qblk, 0.0, -1e30)[None, :]
            gc[:, 1, t * 16:(t + 1) * 16] = (n < qblk).astype(np.float32)[None, :]
            gc[:, 2, t * 16:(t + 1) * 16] = (n == qblk).astype(np.float32)[None, :]
    return cm.astype(ml_dtypes.bfloat16), gc


def make_in_maps(x, positions, w_in, mla_q_norm, mla_kv_norm, w_uq, w_ukv, w_out,
                 ln1_g, ln1_b, w_up, w_down, ln2_g, ln2_b):
    f = np.float32
    w_in = np.asarray(w_in[0], f)
    cq, ckv, kr, mq, mk, mv = np.split(w_in, np.cumsum([384, 256, 64, 1024, 1024])[:], axis=1)
    wA = np.ascontiguousarray(np.concatenate([ckv, kr, kr, mk], axis=1))
    wB = np.ascontiguousarray(mv)
    wC = np.ascontiguousarray(np.concatenate([cq, mq], axis=1))
    wuq0 = np.asarray(w_uq[0], f).reshape(384, 8, 192)
    wuq = np.ascontiguousarray(np.concatenate([wuq0[:, :, :128].reshape(384, 1024), wuq0[:, :, 128:].reshape(384, 512)], axis=1))
    wukv0 = np.asarray(w_ukv[0], f).reshape(256, 8, 256)
    wukv = np.ascontiguousarray(np.concatenate([wukv0[:, :, :128].reshape(256, 1024), wukv0[:, :, 128:].reshape(256, 1024)], axis=1))
    wo = np.asarray(w_out[0], f)
    wout_t = np.ascontiguousarray(wo.reshape(16, 128, 16, 128).transpose(2, 1, 0, 3))
    wu = np.asarray(w_up[0], f)
    wup_t = np.ascontiguousarray(wu.reshape(16, 128, 64, 128).transpose(2, 1, 0, 3))
    wdn = np.asarray(w_down[0], f)
    wdn_t = np.ascontiguousarray(wdn.reshape(8, 8, 128, 16, 128).transpose(0, 3, 2, 1, 4))
    pcol = np.zeros((128, PC_N), f)
    pcol[:, PC_QN:PC_QN + 3] = np.asarray(mla_q_norm[0], f).reshape(3, 128).T
    pcol[:, PC_KVN:PC_KVN + 2] = np.asarray(mla_kv_norm[0], f).reshape(2, 128).T
    pcol[:, PC_L1G:PC_L1G + 16] = np.asarray(ln1_g[0], f).reshape(16, 128).T
    pcol[:, PC_L1B:PC_L1B + 16] = np.asarray(ln1_b[0], f).reshape(16, 128).T
    pcol[:, PC_L2G:PC_L2G + 16] = np.asarray(ln2_g[0], f).reshape(16, 128).T
    pcol[:, PC_L2B:PC_L2B + 16] = np.asarray(ln2_b[0], f).reshape(16, 128).T
    pcol[:, PC_ROPE:PC_ROPE + 8] = _rope_cols()
    cbf = _const_bf16()
    cc = [_core_consts(0), _core_consts(1)]
    x = np.asarray(x, f)
    positions = np.asarray(positions, np.int32)
    in_maps = []
    for core in range(8):
        b, j = core // 2, core % 2
        xTb = np.ascontiguousarray(x[b].T)
        cols = np.concatenate([np.arange(G * 512, (G + 1) * 512) for G in OWN[j]])
        in_maps.append({
            "xT": xTb, "xTo": np.ascontiguousarray(xTb[:, cols]),
            "pos": np.ascontiguousarray(positions[b][None, :]), "poso": np.ascontiguousarray(positions[b][cols][None, :]),
            "wA": wA, "wB": wB, "wC": wC, "wuq": wuq, "wukv": wukv,
            "wout_t": wout_t, "wup_t": wup_t, "wdn_t": wdn_t,
            "pcol": pcol, "cbf": cbf, "cmask": cc[j][0], "gc": cc[j][1],
        })
    return in_maps


_NC_CACHE = {}


def kernel(**inputs):
    in_maps = make_in_maps(**inputs)
    if "nc" not in _NC_CACHE:
        _NC_CACHE["nc"] = build_program()
    nc = _NC_CACHE["nc"]
    res = run_bass_kernel_spmd(nc, in_maps, core_ids=list(range(8)))
    out = np.zeros((NB, S, D), np.float32)
    for core in range(8):
        b, j = core // 2, core % 2
        yT = np.asarray(res.results[core]["yT"])
        cols = np.concatenate([np.arange(G * 512, (G + 1) * 512) for G in OWN[j]])
        out[b, cols, :] = yT.T
    return out
```

```python
import math
from contextlib import ExitStack

import numpy as np
import ml_dtypes

import concourse.bass as bass
import concourse.mybir as mybir
from concourse.bass_utils import run_bass_kernel_spmd

F32 = mybir.dt.float32
BF16 = mybir.dt.bfloat16
I32 = mybir.dt.int32
AF = mybir.ActivationFunctionType
ALU = mybir.AluOpType
AX = mybir.AxisListType

D = 2048
S = 4096
NB = 4
TG = 512
NTG = 8
NSL = 4
OWN = [[0, 3, 4, 7], [1, 2, 5, 6]]
NEG = -30720.0
THETA = 500000.0
ALPHA = 2.0 ** 0.25
MLA_SCALE = 1.0 / math.sqrt(192.0)
MOBA_SCALE = 1.0 / math.sqrt(128.0)
LN_EPS = 1e-5
RMS_EPS = 1e-6
MAGIC = 12582912.0
TWO_PI_S = 6.283185
SB_BASE = 16512
SB_END = 229376 - 256

PC_QN = 0
PC_KVN = 3
PC_L1G = 5
PC_L1B = 21
PC_L2G = 37
PC_L2B = 53
PC_ROPE = 69
PC_N = 77
CB_ONES = 0
CB_ID = 128
CB_PM = 256
CB_PA = 384
CB_E = 512
CB_N = 512 + 2048

ENGS = ("sp", "pool", "pe", "act", "dve")


class Buf:
    __slots__ = ("name", "w", "r")

    def __init__(self, name=""):
        self.name = name
        self.w = []
        self.r = []


class Prog:
    def __init__(self, nc, stack):
        self.nc = nc
        self.stack = stack
        self.q = {e: [] for e in ENGS}
        self.esem = {}
        self.seen = {e: {} for e in ENGS}
        self.dma_pool = {"sp": [], "pool": []}
        self.dma_i = {"sp": 0, "pool": 0}
        self.nsem = 0
        self.NDMA = 12
        for e in ("pe", "act", "dve", "pool"):
            self._new_esem(e)
        for e in ("sp", "pool"):
            for i in range(self.NDMA):
                h = stack.enter_context(nc.semaphore(f"d_{e}_{i}"))
                self.dma_pool[e].append([h, 0, None])

    def _new_esem(self, e):
        self.nsem += 1
        h = self.stack.enter_context(self.nc.semaphore(f"e_{e}_{self.nsem}"))
        self.esem[e] = [h, 0]

    def _waits_for(self, eng, reads, writes, extra):
        toks = []
        for b in reads:
            toks.extend(b.w)
        for b in writes:
            toks.extend(b.w)
            toks.extend(b.r)
        toks.extend(extra)
        best = {}
        for (h, v, src) in toks:
            if src == "pe" and eng == "pe":
                continue
            k = id(h)
            if k not in best or best[k][1] < v:
                best[k] = (h, v)
        out = []
        seen = self.seen[eng]
        for k, (h, v) in best.items():
            if seen.get(k, -1) >= v:
                continue
            seen[k] = v
            out.append((h, v))
        return out

    def op(self, eng, fn, reads=(), writes=(), extra=(), signal=True):
        waits = self._waits_for(eng, reads, writes, extra)
        tok = None
        inc = None
        if signal:
            es = self.esem[eng]
            if es[1] >= 30000:
                self._new_esem(eng)
                es = self.esem[eng]
            es[1] += 1
            tok = (es[0], es[1], eng)
            inc = (es[0], 1)
        self.q[eng].append((waits, fn, inc))
        if tok is not None:
            for b in reads:
                b.r.append(tok)
            for b in writes:
                b.w = [tok]
                b.r = []
        return tok

    def dma(self, eng, fn, reads=(), writes=(), extra=(), commit=True):
        i = self.dma_i[eng] % self.NDMA
        self.dma_i[eng] += 1
        slot = self.dma_pool[eng][i]
        ex = list(extra)
        if slot[2] is not None:
            ex.append(slot[2])
        waits = self._waits_for(eng, reads, writes, ex)
        slot[1] += 16
        tok = (slot[0], slot[1], "dma")
        slot[2] = tok
        self.q[eng].append((waits, fn, (slot[0], 16)))
        if commit:
            for b in reads:
                b.r.append(tok)
            for b in writes:
                b.w = [tok]
                b.r = []
        return tok

    def dma_multi(self, eng, fns, reads=(), writes=()):
        toks = [self.dma(eng, fn, reads=reads, writes=writes, commit=False) for fn in fns]
        for b in reads:
            b.r.extend(toks)
        for b in writes:
            b.w = list(toks)
            b.r = []
        return toks

    def all_tokens(self):
        toks = []
        for e, (h, cnt) in self.esem.items():
            if cnt > 0:
                toks.append((h, cnt, "bar"))
        for e in ("sp", "pool"):
            for slot in self.dma_pool[e]:
                if slot[2] is not None:
                    toks.append((slot[2][0], slot[2][1], "bar"))
        return toks

    def barrier(self):
        toks = self.all_tokens()
        for eng in ENGS:
            waits = self._waits_for(eng, (), (), toks)
            self.q[eng].append((waits, None, None))

    def final_wait(self, eng):
        waits = self._waits_for(eng, (), (), self.all_tokens())
        self.q[eng].append((waits, None, None))

    def emit(self, block):
        def run(e, items):
            for waits, fn, inc in items:
                for (h, v) in waits:
                    e.wait_ge(h, v)
                if fn is None:
                    continue
                ins = fn(e)
                if inc is not None:
                    ins.then_inc(inc[0], inc[1])

        q = self.q

        @block.sync
        def _(e):
            run(e, q["sp"])

        @block.gpsimd
        def _(e):
            run(e, q["pool"])

        @block.tensor
        def _(e):
            run(e, q["pe"])

        @block.scalar
        def _(e):
            run(e, q["act"])

        @block.vector
        def _(e):
            run(e, q["dve"])


class Arena:
    def __init__(self, nc, base, end):
        self.nc = nc
        self.base = base
        self.end = end
        self.off = base
        self.n = 0

    def mark(self):
        return self.off

    def reset(self, mark):
        self.off = mark

    def alloc(self, name, shape, dt):
        nbytes = int(np.prod(shape[1:])) * mybir.dt.size(dt)
        nbytes = (nbytes + 31) // 32 * 32
        assert self.off + nbytes <= self.end, f"SBUF arena overflow at {name}: {self.off + nbytes - self.base}"
        self.n += 1
        t = self.nc.alloc_sbuf_tensor_at(f"{name}_{self.n}", list(shape), dt, offset=self.off)
        self.off += nbytes
        self.hw = max(getattr(self, 'hw', 0), self.off)
        return t


def f_mm(o, l, r, st, sp):
    return lambda e: e.matmul(o, l, r, start=st, stop=sp)


def f_act(o, i, func, scale=None, bias=None):
    kw = {}
    if scale is not None:
        kw["scale"] = scale
    if bias is not None:
        kw["bias"] = bias
    return lambda e: e.activation(out=o, in_=i, func=func, **kw)


def f_copy(o, i):
    return lambda e: e.tensor_copy(out=o, in_=i)


def f_tt(o, a, b, op):
    return lambda e: e.tensor_tensor(out=o, in0=a, in1=b, op=op)


def f_ts(o, a, s1, s2, op0, op1=None):
    if op1 is None:
        return lambda e: e.tensor_scalar(out=o, in0=a, scalar1=s1, scalar2=None, op0=op0)
    return lambda e: e.tensor_scalar(out=o, in0=a, scalar1=s1, scalar2=s2, op0=op0, op1=op1)


def f_stt(o, a, s, b, op0, op1):
    return lambda e: e.scalar_tensor_tensor(out=o, in0=a, scalar=s, in1=b, op0=op0, op1=op1)


def f_red(o, i, op):
    return lambda e: e.tensor_reduce(out=o, in_=i, axis=AX.X, op=op)


def f_recip(o, i):
    return lambda e: e.reciprocal(out=o, in_=i)


def f_memset(o, v):
    return lambda e: e.memset(o, v)


def f_dma(o, i):
    return lambda e: e.dma_start(out=o, in_=i)


def MM(P, outb, out_ap, items):
    n = len(items)
    allb = []
    tok = None
    for i, (l, r, bufs) in enumerate(items):
        for b in bufs:
            if b not in allb:
                allb.append(b)
        last = i == n - 1
        tok = P.op("pe", f_mm(out_ap, l, r, i == 0, last),
                   reads=(allb if last else bufs),
                   writes=([outb] if (i == 0 or last) else ()),
                   signal=last)
    return tok


def build_program(debug=False, phases=("A", "B", "C", "MLA", "MOBA", "POST")):
    nc = bass.Bass("TRN2", target_bir_lowering=False)
    dk = "ExternalOutput" if debug else "Internal"

    def din(name, shape, dt):
        return nc.dram_tensor(name, list(shape), dt, kind="ExternalInput").ap()

    def dscr(name, shape, dt):
        return nc.dram_tensor(name, list(shape), dt, kind=dk).ap()

    xT = din("xT", [D, S], F32)
    xTo = din("xTo", [D, 2048], F32)
    pos = din("pos", [1, S], I32)
    poso = din("poso", [1, 2048], I32)
    wA = din("wA", [D, 1408], F32)
    wB = din("wB", [D, 1024], F32)
    wC = din("wC", [D, 1408], F32)
    wuq_d = din("wuq", [384, 1536], F32)
    wukv_d = din("wukv", [256, 2048], F32)
    wout_t = din("wout_t", [16, 128, 16, 128], F32)
    wup_t = din("wup_t", [64, 128, 16, 128], F32)
    wdn_t = din("wdn_t", [8, 16, 128, 8, 128], F32)
    pcol_d = din("pcol", [128, PC_N], F32)
    cbf_d = din("cbf", [128, CB_N], BF16)
    cmask_d = din("cmask", [4, 8, 128, 512], BF16)
    gc_d = din("gc", [128, 3, 256], F32)
    yT = nc.dram_tensor("yT", [D, 2048], F32, kind="ExternalOutput").ap()

    KTm = dscr("KTm", [8, 128, S], BF16)
    Vm = dscr("Vm", [8, S, 128], BF16)
    QTm = dscr("QTm", [8, 128, 2048], BF16)
    KTa = dscr("KTa", [8, 128, S], BF16)
    KR = dscr("KR", [128, S], BF16)
    Va = dscr("Va", [8, S, 128], BF16)
    QTa = dscr("QTa", [8, 128, 2048], BF16)
    QRa = dscr("QRa", [4, 128, 2048], BF16)
    AT = dscr("AT", [16, 128, 2048], BF16)

    with ExitStack() as st:
        P = Prog(nc, st)
        A = Arena(nc, SB_BASE, SB_END)
        psall = st.enter_context(nc.psum_tensor("psall", [128, 4096], F32))
        ps = [psall[:, i * 512:(i + 1) * 512] for i in range(8)]
        psb = [Buf(f"ps{i}") for i in range(8)]

        pcol = A.alloc("pcol", [128, PC_N + 32], F32)
        b_pcol = Buf("pcol")
        cbf = A.alloc("cbf", [128, CB_N], BF16)
        b_cbf = Buf("cbf")
        kmean = A.alloc("kmean", [128, 128], BF16)
        b_kmean = Buf("kmean")
        ksum = A.alloc("ksum", [128, 128], F32)
        b_ksum = [[Buf() for _ in range(NTG)] for _ in range(8)]
        epsb = A.alloc("epsb", [128, 8], F32)
        b_eps = Buf()
        P.dma("sp", f_dma(pcol[:, 0:PC_N], pcol_d[:, :]), writes=[b_pcol])
        P.dma("sp", f_dma(cbf[:], cbf_d[:, :]), writes=[b_cbf])
        PC_L1GA = PC_N
        PC_L1BA = PC_N + 16
        P.op("dve", f_ts(pcol[:, PC_N:PC_N + 32], pcol[:, PC_L1G:PC_L1G + 32], ALPHA, None, ALU.mult),
             reads=[b_pcol], writes=[b_pcol])
        P.op("dve", f_memset(epsb[:, 0:1], RMS_EPS), writes=[b_eps])
        P.op("dve", f_memset(epsb[:, 1:2], LN_EPS), reads=[b_eps], writes=[b_eps])
        eps_rms = epsb[:, 0:1]
        eps_ln = epsb[:, 1:2]
        ones = cbf[:, CB_ONES:CB_ONES + 128]
        ident = cbf[:, CB_ID:CB_ID + 128]
        Pm = cbf[:, CB_PM:CB_PM + 128]
        Pa = cbf[:, CB_PA:CB_PA + 128]
        gmark = A.mark()

        rr = {"ev": 0, "bank": 0}

        def evac(out_ap, in_ap, reads, writes):
            rr["ev"] += 1
            if rr["ev"] % 2 == 0:
                return P.op("act", f_act(out_ap, in_ap, AF.Copy), reads=reads, writes=writes)
            return P.op("dve", f_copy(out_ap, in_ap), reads=reads, writes=writes)

        def nbank(lo, hi):
            rr["bank"] += 1
            return lo + rr["bank"] % (hi - lo)

        if any(p in phases for p in ("A", "B", "C")):
            Wr = [A.alloc("Wr0", [128, 16, 1408], BF16), A.alloc("Wr1", [128, 16, 1408], BF16)]
            b_Wr = [Buf("Wr0"), Buf("Wr1")]
            wuq = A.alloc("wuq", [128, 3, 1536], BF16)
            wukv = A.alloc("wukv", [128, 2, 2048], BF16)
            b_wuq = Buf("wuq")
            b_wukv = Buf("wukv")
            xb = [A.alloc("xb0", [128, 16, 512], BF16), A.alloc("xb1", [128, 16, 512], BF16)]
            b_xb = [Buf("xb0"), Buf("xb1")]
            posi = A.alloc("posi", [128, 512], I32)
            posf = A.alloc("posf", [128, 512], F32)
            b_posi = Buf()
            b_posf = Buf()
            turn = A.alloc("turn", [128, 512], F32)
            kks = [A.alloc(f"kk{i}", [128, 512], F32) for i in range(4)]
            b_turn = Buf()
            b_kks = [Buf() for _ in range(4)]
            tabs2 = [[A.alloc(f"tab{k}_{i}", [128, 512], F32) for i in range(4)] for k in range(2)]
            b_tabs2 = [[Buf() for _ in range(4)] for _ in range(2)]
            cur = {"set": 0}
            NHB = 3
            hb = [A.alloc(f"hb{i}", [128, 512], BF16) for i in range(NHB)]
            b_hb = [Buf() for _ in range(NHB)]
            t1 = [A.alloc(f"t1_{i}", [128, 512], F32) for i in range(2)]
            b_t1 = [Buf() for _ in range(2)]
            t2 = [A.alloc(f"t2_{i}", [128, 512], F32) for i in range(2)]
            b_t2 = [Buf() for _ in range(2)]
            n32 = t1
            b_n32 = b_t1
            cb = [A.alloc(f"cb{i}", [128, 512], BF16) for i in range(3)]
            b_cb = [Buf() for _ in range(3)]
            sq = [A.alloc(f"sq{i}", [128, 512], BF16) for i in range(3)]
            b_sq = [Buf() for _ in range(3)]
            cn = [A.alloc(f"cn{i}", [128, 512], BF16) for i in range(3)]
            b_cn = [Buf() for _ in range(3)]
            rstd = A.alloc("rstd", [128, 512], F32)
            b_rstd = Buf()
            rt, b_rt = rstd, b_rstd
            NKS = 4
            kst = [A.alloc(f"kst{i}", [128, 512], BF16) for i in range(NKS)]
            b_kst = [Buf() for _ in range(NKS)]
            vst = [A.alloc(f"vst{i}", [128, 1024], BF16) for i in range(2)]
            b_vst = [[Buf(), Buf()] for _ in range(2)]
            cnt = {"hb": 0, "t": 0, "kst": 0, "vst": 0, "xb": 0}

            def load_w(dst_i, src, ncols):
                v = src.rearrange("(c p) n -> p c n", p=128)
                cuts = [0, 384, 768, 1088, 1408] if ncols == 1408 else [0, 256, 512, 768, 1024]
                fns = []
                for g in range(4):
                    for cg in range(2):
                        fns.append(f_dma(Wr[dst_i][:, 8 * cg:8 * cg + 8, cuts[g]:cuts[g + 1]], v[:, 8 * cg:8 * cg + 8, cuts[g]:cuts[g + 1]]))
                P.dma_multi("pool", fns, writes=[b_Wr[dst_i]])

            def load_x(src, col0):
                i = cnt["xb"] % 2
                cnt["xb"] += 1
                v = src.rearrange("(c p) t -> p c t", p=128)
                P.dma_multi("pool", [f_dma(xb[i][:, 4 * g:4 * g + 4, :], v[:, 4 * g:4 * g + 4, col0:col0 + 512]) for g in range(4)],
                            writes=[b_xb[i]])
                return i

            def make_tables(psrc, col0, tset):
                tabs, b_tabs = tabs2[tset], b_tabs2[tset]
                P.dma("sp", f_dma(posi[:], psrc[0:1, col0:col0 + 512].partition_broadcast(128)), writes=[b_posi])
                P.op("dve", f_copy(posf[:], posi[:]), reads=[b_posi], writes=[b_posf])
                for ti in range(4):
                    ci = PC_ROPE + 2 * ti
                    kk, b_kk = kks[ti], b_kks[ti]
                    P.op("dve", f_ts(turn[:], posf[:], pcol[:, ci:ci + 1], pcol[:, ci + 1:ci + 2], ALU.mult, ALU.add),
                         reads=[b_posf, b_pcol], writes=[b_turn])
                    P.op("dve", f_ts(kk[:], turn[:], MAGIC, None, ALU.add), reads=[b_turn], writes=[b_kk])
                    P.op("dve", f_ts(kk[:], kk[:], MAGIC, None, ALU.subtract), reads=[b_kk], writes=[b_kk])
                    P.op("dve", f_tt(kk[:], turn[:], kk[:], ALU.subtract), reads=[b_turn, b_kk], writes=[b_kk])

                def sin_part():
                    for ti in range(4):
                        P.op("act", f_act(tabs[ti][:], kks[ti][:], AF.Sin, scale=TWO_PI_S), reads=[b_kks[ti]], writes=[b_tabs[ti]])
                return sin_part

            def rope_sum(ti, out_ap, out_bufs):
                P.op("pool", f_tt(out_ap, t1[ti][:], t2[ti][:], ALU.add), reads=[b_t1[ti], b_t2[ti]], writes=out_bufs)

            def next_kst():
                i = cnt["kst"] % NKS
                cnt["kst"] += 1
                return i

            def rms_part1(xi, wsrc, b_w, col0, nch):
                for j in range(nch):
                    bk = nbank(0, 5)
                    MM(P, psb[bk], ps[bk][:], [(wsrc[:, c, col0 + j * 128:col0 + (j + 1) * 128], xb[xi][:, c, :],
                                                [b_xb[xi], b_w]) for c in range(16)])
                    P.op("act", f_act(cb[j][:], ps[bk][:], AF.Copy), reads=[psb[bk]], writes=[b_cb[j]])
                    P.op("act", f_act(sq[j][:], ps[bk][:], AF.Square), reads=[psb[bk]], writes=[b_sq[j]])

            def rms_part2(nch, gcol, inv_n):
                sk = nbank(5, 8)
                MM(P, psb[sk], ps[sk][:], [(ones, sq[j][:], [b_sq[j], b_cbf]) for j in range(nch)])
                P.op("act", f_act(rt[:], ps[sk][:], AF.Sqrt, scale=inv_n, bias=eps_rms), reads=[psb[sk], b_eps], writes=[b_rt])
                P.op("dve", f_recip(rstd[:], rt[:]), reads=[b_rt], writes=[b_rstd])
                for j in range(nch):
                    P.op("dve", f_stt(cn[j][:], cb[j][:], pcol[:, gcol + j:gcol + j + 1], rstd[:], ALU.mult, ALU.mult),
                         reads=[b_cb[j], b_rstd, b_pcol], writes=[b_cn[j]])

            def rope_part1(bank, typ):
                hi = cnt["hb"] % NHB
                cnt["hb"] += 1
                P.op("act", f_act(hb[hi][:], ps[bank][:], AF.Copy), reads=[psb[bank]], writes=[b_hb[hi]])
                return (hi, typ)

            def rope_part2(state):
                hi, typ = state
                tabs, b_tabs = tabs2[cur["set"]], b_tabs2[cur["set"]]
                Pmat = Pm if typ == 0 else Pa
                Ct, Sg = tabs[2 * typ], tabs[2 * typ + 1]
                bC, bS = b_tabs[2 * typ], b_tabs[2 * typ + 1]
                ti = cnt["t"] % 2
                cnt["t"] += 1
                xbk = nbank(5, 8)
                MM(P, psb[xbk], ps[xbk][:], [(Pmat, hb[hi][:], [b_hb[hi], b_cbf])])
                P.op("pool", f_tt(t1[ti][:], hb[hi][:], Ct[:], ALU.mult), reads=[b_hb[hi], bC], writes=[b_t1[ti]])
                P.op("dve", f_tt(t2[ti][:], ps[xbk][:], Sg[:], ALU.mult), reads=[psb[xbk], bS], writes=[b_t2[ti]])
                return ti

            P.dma("pool", f_dma(wuq[:], wuq_d.rearrange("(c p) n -> p c n", p=128)), writes=[b_wuq])
            P.dma("pool", f_dma(wukv[:], wukv_d.rearrange("(c p) n -> p c n", p=128)), writes=[b_wukv])

            nxt = None
            if "A" in phases:
                nxt = load_x(xT, 0)
                load_w(0, wA, 1408)
                WA = Wr[0]
                for tg in range(NTG):
                    xi = nxt
                    if tg + 1 < NTG:
                        nxt = load_x(xT, (tg + 1) * 512)
                    elif "B" in phases:
                        load_w(1, wB, 1024)
                        nxt = load_x(xT, 0)
                    if tg == 0:
                        make_tables(pos, 0, 0)()
                    sin_next = None
                    cur["set"] = tg % 2
                    c0 = tg * 512
                    pend = []

                    def up_k(h, c0=c0):
                        bk = nbank(0, 5)
                        MM(P, psb[bk], ps[bk][:], [(wukv[:, r, h * 128:(h + 1) * 128], cn[r][:], [b_cn[r], b_wukv]) for r in range(2)])
                        ki = next_kst()
                        evac(kst[ki][:], ps[bk][:], [psb[bk]], [b_kst[ki]])
                        P.dma("sp", f_dma(KTa[h, :, c0:c0 + 512], kst[ki][:]), reads=[b_kst[ki]])

                    def up_v(tt, c0=c0):
                        vi = cnt["vst"] % 2
                        cnt["vst"] += 1
                        for half in range(2):
                            bk = nbank(0, 5)
                            MM(P, psb[bk], ps[bk][:], [(cn[r][:, tt * 128:(tt + 1) * 128],
                                                        wukv[:, r, 1024 + half * 512:1024 + (half + 1) * 512],
                                                        [b_cn[r], b_wukv]) for r in range(2)])
                            evac(vst[vi][:, half * 512:(half + 1) * 512], ps[bk][:], [psb[bk]], [b_vst[vi][half]])
                        r0 = c0 + tt * 128
                        P.dma("sp", f_dma(Va[:, r0:r0 + 128, :].rearrange("h p d -> p h d"),
                                          vst[vi][:].rearrange("p (h d) -> p h d", h=8)), reads=b_vst[vi])

                    def fin_kr(state, c0=c0):
                        ti = rope_part2(state)
                        ki = next_kst()
                        rope_sum(ti, kst[ki][:], [b_kst[ki]])
                        P.dma("sp", f_dma(KR[:, c0:c0 + 512], kst[ki][:]), reads=[b_kst[ki]])

                    def fin_mk(state, h, tg=tg, c0=c0):
                        ti = rope_part2(state)
                        rope_sum(ti, n32[ti][:], [b_n32[ti]])
                        P.op("dve", f_red(ksum[:, h * 16 + 2 * tg:h * 16 + 2 * tg + 2],
                                          n32[ti][:].rearrange("p (a b) -> p a b", a=2), ALU.add),
                             reads=[b_n32[ti]], writes=[b_ksum[h][tg]])
                        ki = next_kst()
                        P.op("act", f_act(kst[ki][:], n32[ti][:], AF.Copy), reads=[b_n32[ti]], writes=[b_kst[ki]])
                        P.dma("sp", f_dma(KTm[h, :, c0:c0 + 512], kst[ki][:]), reads=[b_kst[ki]])

                    rms_part1(xi, WA, b_Wr[0], 0, 2)
                    bk = nbank(0, 5)
                    MM(P, psb[bk], ps[bk][:], [(WA[:, c, 256:384], xb[xi][:, c, :], [b_xb[xi], b_Wr[0]]) for c in range(16)])
                    st_kr = rope_part1(bk, 1)
                    rms_part2(2, PC_KVN, 1.0 / 256.0)
                    ups = [(lambda h=h: up_k(h)) for h in range(8)] + [(lambda tt=tt: up_v(tt)) for tt in range(4)]
                    prev = lambda: fin_kr(st_kr)
                    for h in range(8):
                        bk = nbank(0, 5)
                        MM(P, psb[bk], ps[bk][:], [(WA[:, c, 384 + h * 128:384 + (h + 1) * 128], xb[xi][:, c, :],
                                                    [b_xb[xi], b_Wr[0]]) for c in range(16)])
                        st_h = rope_part1(bk, 0)
                        prev()
                        prev = (lambda s_=st_h, h=h: fin_mk(s_, h))
                        if h >= 1:
                            for _ in range(2):
                                if ups:
                                    ups.pop(0)()
                        if h == 1 and tg + 1 < NTG:
                            sin_next = make_tables(pos, (tg + 1) * 512, (tg + 1) % 2)
                        if h == 5 and sin_next is not None:
                            sin_next()
                    prev()
                    while ups:
                        ups.pop(0)()
                allks = [b for hh in b_ksum for b in hh]
                P.op("dve", f_ts(kmean[:], ksum[:], 1.0 / 256.0, None, ALU.mult), reads=allks, writes=[b_kmean])

            if "B" in phases:
                if "A" not in phases:
                    load_w(1, wB, 1024)
                    nxt = load_x(xT, 0)
                WB = Wr[1]
                sin_c = None
                if "C" in phases:
                    sin_c = make_tables(poso, 0, 0)
                for tg in range(NTG):
                    if tg == 2 and sin_c is not None:
                        sin_c()
                    xi = nxt
                    if tg + 1 < NTG:
                        nxt = load_x(xT, (tg + 1) * 512)
                    elif "C" in phases:
                        load_w(0, wC, 1408)
                        nxt = load_x(xTo, 0)
                    c0 = tg * 512
                    for tt in range(4):
                        vi = cnt["vst"] % 2
                        cnt["vst"] += 1
                        for half in range(2):
                            bk = nbank(0, 8)
                            MM(P, psb[bk], ps[bk][:], [(xb[xi][:, c, tt * 128:(tt + 1) * 128],
                                                        WB[:, c, half * 512:(half + 1) * 512],
                                                        [b_xb[xi], b_Wr[1]]) for c in range(16)])
                            evac(vst[vi][:, half * 512:(half + 1) * 512], ps[bk][:], [psb[bk]], [b_vst[vi][half]])
                        r0 = c0 + tt * 128
                        P.dma("sp", f_dma(Vm[:, r0:r0 + 128, :].rearrange("h p d -> p h d"),
                                          vst[vi][:].rearrange("p (h d) -> p h d", h=8)), reads=b_vst[vi])

            if "C" in phases:
                if "B" not in phases:
                    load_w(0, wC, 1408)
                    nxt = load_x(xTo, 0)
                WC = Wr[0]
                for s in range(NSL):
                    xi = nxt
                    if s + 1 < NSL:
                        nxt = load_x(xTo, (s + 1) * 512)
                    if s == 0 and "B" not in phases:
                        make_tables(poso, 0, 0)()
                    sin_next = None
                    cur["set"] = s % 2
                    c0 = s * 512
                    def up_qn(h, c0=c0):
                        bk = nbank(0, 5)
                        MM(P, psb[bk], ps[bk][:], [(wuq[:, r, h * 128:(h + 1) * 128], cn[r][:], [b_cn[r], b_wuq]) for r in range(3)])
                        ki = next_kst()
                        evac(kst[ki][:], ps[bk][:], [psb[bk]], [b_kst[ki]])
                        P.dma("sp", f_dma(QTa[h, :, c0:c0 + 512], kst[ki][:]), reads=[b_kst[ki]])

                    def up_qr1(pr):
                        bk = nbank(0, 5)
                        MM(P, psb[bk], ps[bk][:], [(wuq[:, r, 1024 + pr * 128:1024 + (pr + 1) * 128], cn[r][:], [b_cn[r], b_wuq])
                                                    for r in range(3)])
                        return rope_part1(bk, 1)

                    def fin_rope(state, dst):
                        ti = rope_part2(state)
                        ki = next_kst()
                        rope_sum(ti, kst[ki][:], [b_kst[ki]])
                        P.dma("sp", f_dma(dst, kst[ki][:]), reads=[b_kst[ki]])

                    rms_part1(xi, WC, b_Wr[0], 0, 3)
                    prev = None
                    qr_states = []
                    for h in range(8):
                        bk = nbank(0, 5)
                        MM(P, psb[bk], ps[bk][:], [(WC[:, c, 384 + h * 128:384 + (h + 1) * 128], xb[xi][:, c, :],
                                                    [b_xb[xi], b_Wr[0]]) for c in range(16)])
                        st_h = rope_part1(bk, 0)
                        if h == 0:
                            rms_part2(3, PC_QN, 1.0 / 384.0)
                        if prev is not None:
                            prev()
                        prev = (lambda s_=st_h, h=h: fin_rope(s_, QTm[h, :, c0:c0 + 512]))
                        if h >= 1:
                            up_qn(h - 1)
                        if h == 1 and s + 1 < NSL:
                            sin_next = make_tables(poso, (s + 1) * 512, (s + 1) % 2)
                        if h == 5 and sin_next is not None:
                            sin_next()
                        if h >= 2 and h - 2 < 4:
                            pr = h - 2
                            if qr_states:
                                pr0, s0 = qr_states.pop(0)
                                fin_rope(s0, QRa[pr0, :, c0:c0 + 512])
                            qr_states.append((pr, up_qr1(pr)))
                    prev()
                    up_qn(7)
                    while qr_states:
                        pr0, s0 = qr_states.pop(0)
                        fin_rope(s0, QRa[pr0, :, c0:c0 + 512])
            P.barrier()
            A.reset(gmark)

        if "MLA" in phases or "MOBA" in phases:
            cmask = A.alloc("cmask", [128, 32, 512], BF16)
            b_cmask = Buf()
            cmv = cmask_d.rearrange("s i p q -> p (s i) q")
            P.dma_multi("sp", [f_dma(cmask[:, 8 * g:8 * g + 8, :], cmv[:, 8 * g:8 * g + 8, :]) for g in range(4)], writes=[b_cmask])
            KRe = A.alloc("KRe", [128, S], BF16)
            KRo = A.alloc("KRo", [128, S], BF16)
            b_KRe, b_KRo = Buf(), Buf()
            P.op("dve", f_memset(KRe[64:128, :], 0.0), writes=[b_KRe])
            P.op("dve", f_memset(KRo[0:64, :], 0.0), writes=[b_KRo])
            P.dma("sp", f_dma(KRe[0:64, :], KR[0:64, :]), reads=[b_KRe], writes=[b_KRe])
            P.dma("sp", f_dma(KRo[64:128, :], KR[64:128, :]), reads=[b_KRo], writes=[b_KRo])
            zf = A.alloc("zf", [128, 1024], F32)
            b_zf = Buf()
            P.op("dve", f_memset(zf[:], 0.0), writes=[b_zf])
            gcs = A.alloc("gcs", [128, 3, 256], F32)
            b_gcs = Buf()
            P.dma("sp", f_dma(gcs[:], gc_d[:, :, :]), writes=[b_gcs])
            Kh = [A.alloc(f"Kh{i}", [128, S], BF16) for i in range(2)]
            Vh = [A.alloc(f"Vh{i}", [128, 32, 128], BF16) for i in range(2)]
            Qh = [A.alloc(f"Qh{i}", [128, 2048], BF16) for i in range(2)]
            QRh = [A.alloc(f"QRh{i}", [128, 2048], BF16) for i in range(2)]
            b_Kh = [Buf(), Buf()]
            b_Vh = [Buf(), Buf()]
            b_Qh = [Buf(), Buf()]
            b_QRh = [Buf(), Buf()]
            NPT = 6
            PT = [A.alloc(f"PT{i}", [128, 1024], BF16) for i in range(NPT)]
            b_PT = [Buf() for _ in range(NPT)]
            negT = [A.alloc(f"negT{i}", [128, 512], BF16) for i in range(2)]
            b_negT = [Buf(), Buf()]
            for i in range(2):
                P.op("dve", f_memset(negT[i][:], 0.0), writes=[b_negT[i]])
            gm = [A.alloc(f"gm{i}", [128, 256], F32) for i in range(4)]
            b_gm = [Buf() for _ in range(4)]
            mx = A.alloc("mx", [128, 16], F32)
            b_mx = Buf()
            negm = [A.alloc(f"negm{i}", [128, 256], BF16) for i in range(2)]
            b_negm = [Buf(), Buf()]
            rinv = A.alloc("rinv", [128, 512], F32)
            b_rinv = Buf()
            accf = [[A.alloc(f"accf{i}{j}", [128, 1024], F32) for j in range(2)] for i in range(2)]
            b_accf = [[Buf(), Buf()] for _ in range(2)]
            accb = [A.alloc(f"accb{j}", [128, 1024], BF16) for j in range(2)]
            b_accb = [Buf(), Buf()]
            ost = [A.alloc(f"ost{i}", [128, 512], BF16) for i in range(2)]
            b_ost = [Buf(), Buf()]
            c2 = {"pt": 0, "sp": 0, "ol": 0, "ost": 0, "negT": 0}
            SP_ = [(0, 1), (2, 3)]
            OB = [4, 5]
            LBK = 6
            GB = 7

            heads = []
            if "MLA" in phases:
                heads += [("a", h) for h in range(8)]
            if "MOBA" in phases:
                heads += [("m", h) for h in range(8)]

            def load_head(idx):
                typ, h = heads[idx]
                hs = idx % 2
                Ksrc, Vsrc, Qsrc = (KTa, Va, QTa) if typ == "a" else (KTm, Vm, QTm)
                P.dma("sp", f_dma(Qh[hs][:], Qsrc[h, :, :]), writes=[b_Qh[hs]])
                P.dma("sp", f_dma(Kh[hs][:], Ksrc[h, :, :]), writes=[b_Kh[hs]])
                P.dma_multi("sp", [f_dma(Vh[hs][:, q4 * 8:(q4 + 1) * 8, :],
                                         Vsrc[h, q4 * 1024:(q4 + 1) * 1024, :].rearrange("(t p) d -> p t d", p=128))
                                   for q4 in range(4)], writes=[b_Vh[hs]])
                if typ == "a" and h % 2 == 0:
                    pi = (h // 2) % 2
                    P.dma("sp", f_dma(QRh[pi][:], QRa[h // 2, :, :]), writes=[b_QRh[pi]])

            def gating(idx):
                typ, h = heads[idx]
                hs = idx % 2
                for t in range(16):
                    P.op("pe", f_mm(ps[GB][:, t * 16:(t + 1) * 16], Qh[hs][:, t * 128:(t + 1) * 128],
                                    kmean[:, h * 16:(h + 1) * 16], True, True),
                         reads=[b_Qh[hs], b_kmean], writes=[psb[GB]], signal=(t == 15))
                g0, g1, g2, g3 = gm

                def v3(ap):
                    return ap.rearrange("p (t n) -> p t n", t=16)

                def mxb():
                    return mx[:].unsqueeze(2).to_broadcast([128, 16, 16])

                P.op("dve", f_tt(g0[:], ps[GB][:, 0:256], gcs[:, 0, :], ALU.add), reads=[psb[GB], b_gcs], writes=[b_gm[0]])
                src_, bsrc = g0, b_gm[0]
                for it in range(2):
                    P.op("dve", f_red(mx[:], v3(src_[:]), ALU.max), reads=[bsrc], writes=[b_mx])
                    P.op("dve", f_tt(v3(g3[:]), v3(src_[:]), mxb(), ALU.is_equal), reads=[bsrc, b_mx], writes=[b_gm[3]])
                    dst, bdst = (g1, b_gm[1]) if it == 0 else (g2, b_gm[2])
                    P.op("dve", f_stt(dst[:], g3[:], -1e30, src_[:], ALU.mult, ALU.add), reads=[b_gm[3], bsrc], writes=[bdst])
                    src_, bsrc = dst, bdst
                P.op("dve", f_red(mx[:], v3(g2[:]), ALU.max), reads=[b_gm[2]], writes=[b_mx])
                P.op("dve", f_tt(v3(g3[:]), v3(g0[:]), mxb(), ALU.is_ge), reads=[b_gm[0], b_mx], writes=[b_gm[3]])
                P.op("dve", f_tt(g3[:], g3[:], gcs[:, 1, :], ALU.mult), reads=[b_gm[3], b_gcs], writes=[b_gm[3]])
                P.op("dve", f_tt(g3[:], g3[:], gcs[:, 2, :], ALU.add), reads=[b_gm[3], b_gcs], writes=[b_gm[3]])
                P.op("dve", f_ts(negm[hs][:], g3[:], -NEG, NEG, ALU.mult, ALU.add), reads=[b_gm[3]], writes=[b_negm[hs]])

            pend_fin = []

            if heads:
                load_head(0)
                if heads[0][0] == "m":
                    gating(0)
            for idx, (typ, h) in enumerate(heads):
                hs = idx % 2
                if idx + 1 < len(heads):
                    load_head(idx + 1)
                scale = MLA_SCALE if typ == "a" else MOBA_SCALE
                for s in range(NSL):
                    nk = 8 * (s + 1)
                    npair = nk // 2
                    q0 = s * 512
                    ni = None
                    if typ == "m":
                        ni = c2["negT"] % 2
                        c2["negT"] += 1
                        for i4 in range(4):
                            t = 4 * s + i4
                            P.op("pe", f_mm(ps[GB][0:16, i4 * 128:(i4 + 1) * 128], negm[hs][:, t * 16:(t + 1) * 16], ident, True, True),
                                 reads=[b_negm[hs], b_cbf], writes=[psb[GB]], signal=(i4 == 3))
                        P.op("act", f_act(negT[ni][0:16, :], ps[GB][0:16, :], AF.Copy), reads=[psb[GB]], writes=[b_negT[ni]])
                    oi = c2["ol"] % 2
                    c2["ol"] += 1
                    ob = OB[oi]

                    def emit_qk_pair(pp):
                        b0, b1 = SP_[c2["sp"] % 2]
                        c2["sp"] += 1
                        for j, bk in enumerate((b0, b1)):
                            kt = 2 * pp + j
                            items = [(Kh[hs][:, kt * 128:(kt + 1) * 128], Qh[hs][:, q0:q0 + 512], [b_Kh[hs], b_Qh[hs]])]
                            if typ == "a":
                                pi = (h // 2) % 2
                                KRx, b_KRx = (KRe, b_KRe) if h % 2 == 0 else (KRo, b_KRo)
                                items.append((KRx[:, kt * 128:(kt + 1) * 128], QRh[pi][:, q0:q0 + 512], [b_KRx, b_QRh[pi]]))
                            else:
                                n = kt // 2
                                items.append((cbf[:, CB_E + n * 128:CB_E + (n + 1) * 128], negT[ni][:], [b_cbf, b_negT[ni]]))
                            if kt >= nk - 8:
                                items.append((ident, cmask[:, s * 8 + kt - (nk - 8), :], [b_cbf, b_cmask]))
                            MM(P, psb[bk], ps[bk][:], items)
                        return (b0, b1)

                    banks = [emit_qk_pair(0), emit_qk_pair(1)]
                    for pp in range(npair):
                        b0, b1 = banks[pp]
                        pi_ = c2["pt"] % NPT
                        c2["pt"] += 1
                        P.op("act", f_act(PT[pi_][:], psall[:, b0 * 512:b0 * 512 + 1024], AF.Exp, scale=scale),
                             reads=[psb[b0], psb[b1]], writes=[b_PT[pi_]])
                        if pp + 2 < npair:
                            banks.append(emit_qk_pair(pp + 2))
                        for j in range(2):
                            kt = 2 * pp + j
                            first, last = kt == 0, kt == nk - 1
                            P.op("pe", f_mm(ps[ob][:], Vh[hs][:, kt, :], PT[pi_][:, j * 512:(j + 1) * 512], first, last),
                                 reads=[b_PT[pi_], b_Vh[hs]], writes=([psb[ob]] if (first or last) else []), signal=(j == 1))
                        ai = pp % 2
                        aeng = "pool" if ai == 0 else "dve"
                        if pp < 2:
                            P.op(aeng, f_tt(accf[oi][ai][:], zf[:], PT[pi_][:], ALU.add), reads=[b_PT[pi_], b_zf], writes=[b_accf[oi][ai]])
                        else:
                            P.op(aeng, f_tt(accf[oi][ai][:], accf[oi][ai][:], PT[pi_][:], ALU.add),
                                 reads=[b_PT[pi_], b_accf[oi][ai]], writes=[b_accf[oi][ai]])
                        if pp >= 1 and pend_fin:
                            pend_fin.pop(0)()
                        if pp == 1 and s == 1 and idx + 1 < len(heads) and heads[idx + 1][0] == "m":
                            gating(idx + 1)

                    while len(pend_fin) > 0:
                        pend_fin.pop(0)()

                    def fin1(oi=oi):
                        for j in range(2):
                            P.op("act", f_act(accb[j][:], accf[oi][j][:], AF.Copy), reads=[b_accf[oi][j]], writes=[b_accb[j]])

                    def fin2():
                        MM(P, psb[LBK], ps[LBK][:], [(ones, accb[j][:, hh * 512:(hh + 1) * 512], [b_accb[j], b_cbf])
                                                      for j in range(2) for hh in range(2)])

                    def fin3():
                        P.op("act", f_act(rinv[:], ps[LBK][:], AF.Ln), reads=[psb[LBK]], writes=[b_rinv])
                        P.op("act", f_act(rinv[:], rinv[:], AF.Exp, scale=-1.0), reads=[b_rinv], writes=[b_rinv])

                    def fin4(ob=ob, q0=q0, ch=(h if typ == "a" else 8 + h)):
                        oi2 = c2["ost"] % 2
                        c2["ost"] += 1
                        P.op("dve", f_tt(ost[oi2][:], ps[ob][:], rinv[:], ALU.mult), reads=[psb[ob], b_rinv], writes=[b_ost[oi2]])
                        P.dma("sp", f_dma(AT[ch, :, q0:q0 + 512], ost[oi2][:]), reads=[b_ost[oi2]])

                    pend_fin.extend([fin1, fin2, fin3, fin4])
            while pend_fin:
                pend_fin.pop(0)()
            P.barrier()
            A.reset(gmark)

        if "POST" in phases:
            R = A.alloc("R", [128, 16, 1024], F32)
            b_R = [[Buf(), Buf()] for _ in range(16)]
            x1T = A.alloc("x1T", [128, 16, 1024], BF16)
            b_x1 = [[Buf(), Buf()] for _ in range(16)]
            asl = A.alloc("asl", [128, 16, 512], BF16)
            b_asl = [Buf() for _ in range(16)]
            hid = A.alloc("hid", [128, 8, 1024], BF16)
            b_hid = [[Buf(), Buf()] for _ in range(8)]
            hid_ln = hid[:].rearrange("p k (a t) -> p (k a) t", a=2)
            NW = 4
            wt = [A.alloc(f"wt{i}", [128, 16, 128], BF16) for i in range(NW)]
            b_wt = [Buf() for _ in range(NW)]
            NWD = 6
            wd = [A.alloc(f"wd{i}", [128, 8, 128], BF16) for i in range(NWD)]
            b_wd = [Buf() for _ in range(NWD)]
            rl = [A.alloc(f"rl{i}", [128, 512], F32) for i in range(2)]
            b_rl = [Buf(), Buf()]
            mean = A.alloc("mean", [128, 512], F32)
            msq = A.alloc("msq", [128, 512], F32)
            var = A.alloc("var", [128, 512], F32)
            rs2 = A.alloc("rs2", [128, 512], F32)
            nmr = A.alloc("nmr", [128, 512], F32)
            b_mean, b_msq, b_var, b_rs2, b_nmr = Buf(), Buf(), Buf(), Buf(), Buf()
            tA = [A.alloc(f"tA{i}", [128, 512], F32) for i in range(2)]
            b_tA = [Buf(), Buf()]
            tB = [A.alloc(f"tB{i}", [128, 512], F32) for i in range(2)]
            b_tB = [Buf(), Buf()]
            c3 = {"wt": 0, "wd": 0, "rl": 0, "t": 0, "bank": 0}

            def pbank():
                c3["bank"] += 1
                return c3["bank"] % 6

            def layer_norm(half, gcol, bcol, first):
                hs_ = slice(half * 512, (half + 1) * 512)
                for c in range(16):
                    P.op("act", f_act(asl[:, c, :], R[:, c, hs_], AF.Copy), reads=[b_R[c][half]], writes=[b_asl[c]])
                    P.op("dve", f_tt(hid_ln[:, c, :], R[:, c, hs_], R[:, c, hs_], ALU.mult), reads=[b_R[c][half]], writes=[b_hid[c // 2][c % 2]])
                MM(P, psb[6], ps[6][:], [(ones, asl[:, c, :], [b_asl[c], b_cbf]) for c in range(16)])
                MM(P, psb[7], ps[7][:], [(ones, hid_ln[:, c, :], [b_hid[c // 2][c % 2], b_cbf]) for c in range(16)])
                P.op("dve", f_ts(mean[:], ps[6][:], 1.0 / D, None, ALU.mult), reads=[psb[6]], writes=[b_mean])
                P.op("dve", f_tt(msq[:], mean[:], mean[:], ALU.mult), reads=[b_mean], writes=[b_msq])
                P.op("dve", f_stt(var[:], ps[7][:], 1.0 / D, msq[:], ALU.mult, ALU.subtract), reads=[psb[7], b_msq], writes=[b_var])
                P.op("act", f_act(var[:], var[:], AF.Sqrt, scale=1.0, bias=eps_ln), reads=[b_var, b_eps], writes=[b_var])
                P.op("dve", f_recip(rs2[:], var[:]), reads=[b_var], writes=[b_rs2])
                P.op("dve", f_stt(nmr[:], mean[:], -1.0, rs2[:], ALU.mult, ALU.mult), reads=[b_mean, b_rs2], writes=[b_nmr])
                for c in range(16):
                    i = c3["t"] % 2
                    c3["t"] += 1
                    P.op("dve", f_tt(tA[i][:], R[:, c, hs_], rs2[:], ALU.mult), reads=[b_R[c][half], b_rs2], writes=[b_tA[i]])
                    P.op("dve", f_tt(tB[i][:], tA[i][:], nmr[:], ALU.add), reads=[b_tA[i], b_nmr], writes=[b_tB[i]])
                    if first:
                        P.op("act", f_act(x1T[:, c, hs_], tB[i][:], AF.Identity, scale=pcol[:, gcol + c:gcol + c + 1],
                                          bias=pcol[:, bcol + c:bcol + c + 1]), reads=[b_tB[i], b_pcol], writes=[b_x1[c][half]])
                        P.op("act", f_act(R[:, c, hs_], tB[i][:], AF.Identity, scale=pcol[:, PC_L1GA + c:PC_L1GA + c + 1],
                                          bias=pcol[:, PC_L1BA + c:PC_L1BA + c + 1]), reads=[b_tB[i], b_pcol], writes=[b_R[c][half]])
                    else:
                        P.op("act", f_act(R[:, c, hs_], tB[i][:], AF.Identity, scale=pcol[:, gcol + c:gcol + c + 1],
                                          bias=pcol[:, bcol + c:bcol + c + 1]), reads=[b_tB[i], b_pcol], writes=[b_R[c][half]])

            xov = xTo.rearrange("(c p) t -> p c t", p=128)
            yv = yT.rearrange("(c p) t -> p c t", p=128)
            atv = AT.rearrange("c p t -> p c t")
            for bl in range(2):
                asrc = [asl, hid_ln]
                b_asrc = [b_asl, [b_hid[c // 2][c % 2] for c in range(16)]]
                for half in range(2):
                    s = 2 * bl + half
                    q0 = s * 512
                    hs_ = slice(half * 512, (half + 1) * 512)
                    P.dma_multi("sp", [f_dma(asrc[half][:, 4 * g:4 * g + 4, :], atv[:, 4 * g:4 * g + 4, q0:q0 + 512]) for g in range(4)],
                                writes=b_asrc[half])
                    P.dma_multi("sp", [f_dma(R[:, 4 * g:4 * g + 4, hs_], xov[:, 4 * g:4 * g + 4, q0:q0 + 512]) for g in range(4)],
                                writes=[b_R[c][half] for c in range(16)])
                for dc in range(16):
                    wi = c3["wt"] % NW
                    c3["wt"] += 1
                    P.dma("pool", f_dma(wt[wi][:], wout_t[dc, :, :, :]), writes=[b_wt[wi]])
                    for half in range(2):
                        hs_ = slice(half * 512, (half + 1) * 512)
                        bk = pbank()
                        MM(P, psb[bk], ps[bk][:], [(wt[wi][:, c, :], asrc[half][:, c, :], [b_wt[wi], b_asrc[half][c]]) for c in range(16)])
                        P.op("dve", f_stt(R[:, dc, hs_], R[:, dc, hs_], ALPHA, ps[bk][:], ALU.mult, ALU.add),
                             reads=[psb[bk], b_R[dc][half]], writes=[b_R[dc][half]])
                for half in range(2):
                    layer_norm(half, PC_L1G, PC_L1B, True)
                for fb in range(8):
                    for fc in range(8):
                        wi = c3["wt"] % NW
                        c3["wt"] += 1
                        P.dma("pool", f_dma(wt[wi][:], wup_t[fb * 8 + fc, :, :, :]), writes=[b_wt[wi]])
                        for half in range(2):
                            hs_ = slice(half * 512, (half + 1) * 512)
                            bk = pbank()
                            MM(P, psb[bk], ps[bk][:], [(wt[wi][:, c, :], x1T[:, c, hs_], [b_wt[wi], b_x1[c][half]]) for c in range(16)])
                            ri = c3["rl"] % 2
                            c3["rl"] += 1
                            P.op("act", f_act(rl[ri][:], ps[bk][:], AF.Relu), reads=[psb[bk]], writes=[b_rl[ri]])
                            P.op("dve", f_tt(hid[:, fc, hs_], rl[ri][:], rl[ri][:], ALU.mult), reads=[b_rl[ri]], writes=[b_hid[fc][half]])
                    for dc in range(16):
                        wi = c3["wd"] % NWD
                        c3["wd"] += 1
                        P.dma("pool", f_dma(wd[wi][:], wdn_t[fb, dc, :, :, :]), writes=[b_wd[wi]])
                        for half in range(2):
                            hs_ = slice(half * 512, (half + 1) * 512)
                            bk = pbank()
                            MM(P, psb[bk], ps[bk][:], [(wd[wi][:, k, :], hid[:, k, hs_], [b_wd[wi], b_hid[k][half]]) for k in range(8)])
                            P.op("dve", f_tt(R[:, dc, hs_], R[:, dc, hs_], ps[bk][:], ALU.add),
                                 reads=[psb[bk], b_R[dc][half]], writes=[b_R[dc][half]])
                for half in range(2):
                    s = 2 * bl + half
                    q0 = s * 512
                    hs_ = slice(half * 512, (half + 1) * 512)
                    layer_norm(half, PC_L2G, PC_L2B, False)
                    P.dma_multi("sp", [f_dma(yv[:, 4 * g:4 * g + 4, q0:q0 + 512], R[:, 4 * g:4 * g + 4, hs_]) for g in range(4)],
                                reads=[b_R[c][half] for c in range(16)])

        for eng in ENGS:
            P.final_wait(eng)
        with nc.Block() as block:
            P.emit(block)
    return nc


def _rope_cols():
    cols = np.zeros((128, 8), np.float32)
    f16 = (THETA ** (-(np.arange(16, dtype=np.float32)) * np.float32(2.0 / 32))).astype(np.float32)
    inv_m = np.zeros(128, np.float32)
    inv_m[0:16] = f16
    inv_m[16:32] = f16
    inv_m = (inv_m.astype(np.float64) / (2 * math.pi)).astype(np.float32)
    cols[:, 0] = inv_m
    cols[:, 1] = 0.25
    cols[:, 2] = inv_m
    offs = np.zeros(128, np.float32)
    offs[0:16] = 0.5
    cols[:, 3] = offs
    f32_ = (THETA ** (-(np.arange(32, dtype=np.float32)) * np.float32(2.0 / 64))).astype(np.float32)
    inv_a = np.concatenate([f32_, f32_, f32_, f32_]).astype(np.float64) / (2 * math.pi)
    cols[:, 4] = inv_a.astype(np.float32)
    cols[:, 5] = 0.25
    cols[:, 6] = inv_a.astype(np.float32)
    offa = np.zeros(128, np.float32)
    offa[0:32] = 0.5
    offa[64:96] = 0.5
    cols[:, 7] = offa
    return cols


def _const_bf16():
    c = np.zeros((128, CB_N), np.float32)
    c[:, CB_ONES:CB_ONES + 128] = 1.0
    c[:, CB_ID:CB_ID + 128] = np.eye(128, dtype=np.float32)
    pm = np.zeros((128, 128), np.float32)
    for m in range(32):
        src = m + 16 if m < 16 else m - 16
        pm[src, m] = 1.0
    c[:, CB_PM:CB_PM + 128] = pm
    pa = np.zeros((128, 128), np.float32)
    for m in range(128):
        src = m + 32 if (m % 64) < 32 else m - 32
        pa[src, m] = 1.0
    c[:, CB_PA:CB_PA + 128] = pa
    for n in range(16):
        c[n, CB_E + n * 128:CB_E + (n + 1) * 128] = 1.0
    return c.astype(ml_dtypes.bfloat16)


def _core_consts(j):
    cm = np.zeros((4, 8, 128, 512), np.float32)
    gc = np.zeros((128, 3, 256), np.float32)
    for s in range(4):
        G = OWN[j][s]
        nk = 8 * (s + 1)
        qpos = G * 512 + np.arange(512)
        for i in range(8):
            kt = nk - 8 + i
            kpos = kt * 128 + np.arange(128)
            cm[s, i] = np.where(kpos[:, None] <= qpos[None, :], 0.0, NEG)
        for i4 in range(4):
            t = 4 * s + i4
            gt = G * 4 + i4
            qblk = gt // 2
            n = np.arange(16)
            gc[:, 0, t * 16:(t + 1) * 16] = np.where(n < qblk, 0.0, -1e30)[None, :]
            gc[:, 1, t * 16:(t + 1) * 16] = (n < qblk).astype(np.float32)[None, :]
            gc[:, 2, t * 16:(t + 1) * 16] = (n == qblk).astype(np.float32)[None, :]
    return cm.astype(ml_dtypes.bfloat16), gc


def make_in_maps(x, positions, w_in, mla_q_norm, mla_kv_norm, w_uq, w_ukv, w_out,
                 ln1_g, ln1_b, w_up, w_down, ln2_g, ln2_b):
    f = np.float32
    w_in = np.asarray(w_in[0], f)
    cq, ckv, kr, mq, mk, mv = np.split(w_in, np.cumsum([384, 256, 64, 1024, 1024])[:], axis=1)
    wA = np.ascontiguousarray(np.concatenate([ckv, kr, kr, mk], axis=1))
    wB = np.ascontiguousarray(mv)
    wC = np.ascontiguousarray(np.concatenate([cq, mq], axis=1))
    wuq0 = np.asarray(w_uq[0], f).reshape(384, 8, 192)
    wuq = np.ascontiguousarray(np.concatenate([wuq0[:, :, :128].reshape(384, 1024), wuq0[:, :, 128:].reshape(384, 512)], axis=1))
    wukv0 = np.asarray(w_ukv[0], f).reshape(256, 8, 256)
    wukv = np.ascontiguousarray(np.concatenate([wukv0[:, :, :128].reshape(256, 1024), wukv0[:, :, 128:].reshape(256, 1024)], axis=1))
    wo = np.asarray(w_out[0], f)
    wout_t = np.ascontiguousarray(wo.reshape(16, 128, 16, 128).transpose(2, 1, 0, 3))
    wu = np.asarray(w_up[0], f)
    wup_t = np.ascontiguousarray(wu.reshape(16, 128, 64, 128).transpose(2, 1, 0, 3))
    wdn = np.asarray(w_down[0], f)
    wdn_t = np.ascontiguousarray(wdn.reshape(8, 8, 128, 16, 128).transpose(0, 3, 2, 1, 4))
    pcol = np.zeros((128, PC_N), f)
    pcol[:, PC_QN:PC_QN + 3] = np.asarray(mla_q_norm[0], f).reshape(3, 128).T
    pcol[:, PC_KVN:PC_KVN + 2] = np.asarray(mla_kv_norm[0], f).reshape(2, 128).T
    pcol[:, PC_L1G:PC_L1G + 16] = np.asarray(ln1_g[0], f).reshape(16, 128).T
    pcol[:, PC_L1B:PC_L1B + 16] = np.asarray(ln1_b[0], f).reshape(16, 128).T
    pcol[:, PC_L2G:PC_L2G + 16] = np.asarray(ln2_g[0], f).reshape(16, 128).T
    pcol[:, PC_L2B:PC_L2B + 16] = np.asarray(ln2_b[0], f).reshape(16, 128).T
    pcol[:, PC_ROPE:PC_ROPE + 8] = _rope_cols()
    cbf = _const_bf16()
    cc = [_core_consts(0), _core_consts(1)]
    x = np.asarray(x, f)
    positions = np.asarray(positions, np.int32)
    in_maps = []
    for core in range(8):
        b, j = core // 2, core % 2
        xTb = np.ascontiguousarray(x[b].T)
        cols = np.concatenate([np.arange(G * 512, (G + 1) * 512) for G in OWN[j]])
        in_maps.append({
            "xT": xTb, "xTo": np.ascontiguousarray(xTb[:, cols]),
            "pos": np.ascontiguousarray(positions[b][None, :]), "poso": np.ascontiguousarray(positions[b][cols][None, :]),
            "wA": wA, "wB": wB, "wC": wC, "wuq": wuq, "wukv": wukv,
            "wout_t": wout_t, "wup_t": wup_t, "wdn_t": wdn_t,
            "pcol": pcol, "cbf": cbf, "cmask": cc[j][0], "gc": cc[j][1],
        })
    return in_maps


_NC_CACHE = {}


def kernel(**inputs):
    in_maps = make_in_maps(**inputs)
    if "nc" not in _NC_CACHE:
        _NC_CACHE["nc"] = build_program()
    nc = _NC_CACHE["nc"]
    res = run_bass_kernel_spmd(nc, in_maps, core_ids=list(range(8)))
    out = np.zeros((NB, S, D), np.float32)
    for core in range(8):
        b, j = core // 2, core % 2
        yT = np.asarray(res.results[core]["yT"])
        cols = np.concatenate([np.arange(G * 512, (G + 1) * 512) for G in OWN[j]])
        out[b, cols, :] = yT.T
    return out
```

```python
import math
from contextlib import ExitStack

import numpy as np
import ml_dtypes

import concourse.bass as bass
import concourse.mybir as mybir
from concourse.bass_utils import run_bass_kernel_spmd

F32 = mybir.dt.float32
BF16 = mybir.dt.bfloat16
I32 = mybir.dt.int32
AF = mybir.ActivationFunctionType
ALU = mybir.AluOpType
AX = mybir.AxisListType

D = 2048
S = 4096
NB = 4
TG = 512
NTG = 8
NSL = 4
OWN = [[0, 3, 4, 7], [1, 2, 5, 6]]
NEG = -30720.0
THETA = 500000.0
ALPHA = 2.0 ** 0.25
MLA_SCALE = 1.0 / math.sqrt(192.0)
MOBA_SCALE = 1.0 / math.sqrt(128.0)
LN_EPS = 1e-5
RMS_EPS = 1e-6
MAGIC = 12582912.0
TWO_PI_S = 6.283185
SB_BASE = 16512
SB_END = 229376 - 256

PC_QN = 0
PC_KVN = 3
PC_L1G = 5
PC_L1B = 21
PC_L2G = 37
PC_L2B = 53
PC_ROPE = 69
PC_N = 77
CB_ONES = 0
CB_ID = 128
CB_PM = 256
CB_PA = 384
CB_E = 512
CB_N = 512 + 2048

ENGS = ("sp", "pool", "pe", "act", "dve")


class Buf:
    __slots__ = ("name", "w", "r")

    def __init__(self, name=""):
        self.name = name
        self.w = []
        self.r = []


class Prog:
    def __init__(self, nc, stack):
        self.nc = nc
        self.stack = stack
        self.q = {e: [] for e in ENGS}
        self.esem = {}
        self.seen = {e: {} for e in ENGS}
        self.dma_pool = {"sp": [], "pool": []}
        self.dma_i = {"sp": 0, "pool": 0}
        self.nsem = 0
        self.NDMA = 12
        for e in ("pe", "act", "dve", "pool"):
            self._new_esem(e)
        for e in ("sp", "pool"):
            for i in range(self.NDMA):
                h = stack.enter_context(nc.semaphore(f"d_{e}_{i}"))
                self.dma_pool[e].append([h, 0, None])

    def _new_esem(self, e):
        self.nsem += 1
        h = self.stack.enter_context(self.nc.semaphore(f"e_{e}_{self.nsem}"))
        self.esem[e] = [h, 0]

    def _waits_for(self, eng, reads, writes, extra):
        toks = []
        for b in reads:
            toks.extend(b.w)
        for b in writes:
            toks.extend(b.w)
            toks.extend(b.r)
        toks.extend(extra)
        best = {}
        for (h, v, src) in toks:
            if src == "pe" and eng == "pe":
                continue
            k = id(h)
            if k not in best or best[k][1] < v:
                best[k] = (h, v)
        out = []
        seen = self.seen[eng]
        for k, (h, v) in best.items():
            if seen.get(k, -1) >= v:
                continue
            seen[k] = v
            out.append((h, v))
        return out

    def op(self, eng, fn, reads=(), writes=(), extra=(), signal=True):
        waits = self._waits_for(eng, reads, writes, extra)
        tok = None
        inc = None
        if signal:
            es = self.esem[eng]
            if es[1] >= 30000:
                self._new_esem(eng)
                es = self.esem[eng]
            es[1] += 1
            tok = (es[0], es[1], eng)
            inc = (es[0], 1)
        self.q[eng].append((waits, fn, inc))
        if tok is not None:
            for b in reads:
                b.r.append(tok)
            for b in writes:
                b.w = [tok]
                b.r = []
        return tok

    def dma(self, eng, fn, reads=(), writes=(), extra=(), commit=True):
        i = self.dma_i[eng] % self.NDMA
        self.dma_i[eng] += 1
        slot = self.dma_pool[eng][i]
        ex = list(extra)
        if slot[2] is not None:
            ex.append(slot[2])
        waits = self._waits_for(eng, reads, writes, ex)
        slot[1] += 16
        tok = (slot[0], slot[1], "dma")
        slot[2] = tok
        self.q[eng].append((waits, fn, (slot[0], 16)))
        if commit:
            for b in reads:
                b.r.append(tok)
            for b in writes:
                b.w = [tok]
                b.r = []
        return tok

    def dma_multi(self, eng, fns, reads=(), writes=()):
        toks = [self.dma(eng, fn, reads=reads, writes=writes, commit=False) for fn in fns]
        for b in reads:
            b.r.extend(toks)
        for b in writes:
            b.w = list(toks)
            b.r = []
        return toks

    def all_tokens(self):
        toks = []
        for e, (h, cnt) in self.esem.items():
            if cnt > 0:
                toks.append((h, cnt, "bar"))
        for e in ("sp", "pool"):
            for slot in self.dma_pool[e]:
                if slot[2] is not None:
                    toks.append((slot[2][0], slot[2][1], "bar"))
        return toks

    def barrier(self):
        toks = self.all_tokens()
        for eng in ENGS:
            waits = self._waits_for(eng, (), (), toks)
            self.q[eng].append((waits, None, None))

    def final_wait(self, eng):
        waits = self._waits_for(eng, (), (), self.all_tokens())
        self.q[eng].append((waits, None, None))

    def emit(self, block):
        def run(e, items):
            for waits, fn, inc in items:
                for (h, v) in waits:
                    e.wait_ge(h, v)
                if fn is None:
                    continue
                ins = fn(e)
                if inc is not None:
                    ins.then_inc(inc[0], inc[1])

        q = self.q

        @block.sync
        def _(e):
            run(e, q["sp"])

        @block.gpsimd
        def _(e):
            run(e, q["pool"])

        @block.tensor
        def _(e):
            run(e, q["pe"])

        @block.scalar
        def _(e):
            run(e, q["act"])

        @block.vector
        def _(e):
            run(e, q["dve"])


class Arena:
    def __init__(self, nc, base, end):
        self.nc = nc
        self.base = base
        self.end = end
        self.off = base
        self.n = 0

    def mark(self):
        return self.off

    def reset(self, mark):
        self.off = mark

    def alloc(self, name, shape, dt):
        nbytes = int(np.prod(shape[1:])) * mybir.dt.size(dt)
        nbytes = (nbytes + 31) // 32 * 32
        assert self.off + nbytes <= self.end, f"SBUF arena overflow at {name}: {self.off + nbytes - self.base}"
        self.n += 1
        t = self.nc.alloc_sbuf_tensor_at(f"{name}_{self.n}", list(shape), dt, offset=self.off)
        self.off += nbytes
        self.hw = max(getattr(self, 'hw', 0), self.off)
        return t


def f_mm(o, l, r, st, sp):
    return lambda e: e.matmul(o, l, r, start=st, stop=sp)


def f_act(o, i, func, scale=None, bias=None):
    kw = {}
    if scale is not None:
        kw["scale"] = scale
    if bias is not None:
        kw["bias"] = bias
    return lambda e: e.activation(out=o, in_=i, func=func, **kw)


def f_copy(o, i):
    return lambda e: e.tensor_copy(out=o, in_=i)


def f_tt(o, a, b, op):
    return lambda e: e.tensor_tensor(out=o, in0=a, in1=b, op=op)


def f_ts(o, a, s1, s2, op0, op1=None):
    if op1 is None:
        return lambda e: e.tensor_scalar(out=o, in0=a, scalar1=s1, scalar2=None, op0=op0)
    return lambda e: e.tensor_scalar(out=o, in0=a, scalar1=s1, scalar2=s2, op0=op0, op1=op1)


def f_stt(o, a, s, b, op0, op1):
    return lambda e: e.scalar_tensor_tensor(out=o, in0=a, scalar=s, in1=b, op0=op0, op1=op1)


def f_red(o, i, op):
    return lambda e: e.tensor_reduce(out=o, in_=i, axis=AX.X, op=op)


def f_recip(o, i):
    return lambda e: e.reciprocal(out=o, in_=i)


def f_memset(o, v):
    return lambda e: e.memset(o, v)


def f_dma(o, i):
    return lambda e: e.dma_start(out=o, in_=i)


def MM(P, outb, out_ap, items):
    n = len(items)
    allb = []
    tok = None
    for i, (l, r, bufs) in enumerate(items):
        for b in bufs:
            if b not in allb:
                allb.append(b)
        last = i == n - 1
        tok = P.op("pe", f_mm(out_ap, l, r, i == 0, last),
                   reads=(allb if last else bufs),
                   writes=([outb] if (i == 0 or last) else ()),
                   signal=last)
    return tok


def build_program(debug=False, phases=("A", "B", "C", "MLA", "MOBA", "POST")):
    nc = bass.Bass("TRN2", target_bir_lowering=False)
    dk = "ExternalOutput" if debug else "Internal"

    def din(name, shape, dt):
        return nc.dram_tensor(name, list(shape), dt, kind="ExternalInput").ap()

    def dscr(name, shape, dt):
        return nc.dram_tensor(name, list(shape), dt, kind=dk).ap()

    xT = din("xT", [D, S], F32)
    xTo = din("xTo", [D, 2048], F32)
    pos = din("pos", [1, S], I32)
    poso = din("poso", [1, 2048], I32)
    wA = din("wA", [D, 1408], F32)
    wB = din("wB", [D, 1024], F32)
    wC = din("wC", [D, 1408], F32)
    wuq_d = din("wuq", [384, 1536], F32)
    wukv_d = din("wukv", [256, 2048], F32)
    wout_t = din("wout_t", [16, 128, 16, 128], F32)
    wup_t = din("wup_t", [64, 128, 16, 128], F32)
    wdn_t = din("wdn_t", [8, 16, 128, 8, 128], F32)
    pcol_d = din("pcol", [128, PC_N], F32)
    cbf_d = din("cbf", [128, CB_N], BF16)
    cmask_d = din("cmask", [4, 8, 128, 512], BF16)
    gc_d = din("gc", [128, 3, 256], F32)
    yT = nc.dram_tensor("yT", [D, 2048], F32, kind="ExternalOutput").ap()

    KTm = dscr("KTm", [8, 128, S], BF16)
    Vm = dscr("Vm", [8, S, 128], BF16)
    QTm = dscr("QTm", [8, 128, 2048], BF16)
    KTa = dscr("KTa", [8, 128, S], BF16)
    KR = dscr("KR", [128, S], BF16)
    Va = dscr("Va", [8, S, 128], BF16)
    QTa = dscr("QTa", [8, 128, 2048], BF16)
    QRa = dscr("QRa", [4, 128, 2048], BF16)
    AT = dscr("AT", [16, 128, 2048], BF16)

    with ExitStack() as st:
        P = Prog(nc, st)
        A = Arena(nc, SB_BASE, SB_END)
        psall = st.enter_context(nc.psum_tensor("psall", [128, 4096], F32))
        ps = [psall[:, i * 512:(i + 1) * 512] for i in range(8)]
        psb = [Buf(f"ps{i}") for i in range(8)]

        pcol = A.alloc("pcol", [128, PC_N + 32], F32)
        b_pcol = Buf("pcol")
        cbf = A.alloc("cbf", [128, CB_N], BF16)
        b_cbf = Buf("cbf")
        kmean = A.alloc("kmean", [128, 128], BF16)
        b_kmean = Buf("kmean")
        ksum = A.alloc("ksum", [128, 128], F32)
        b_ksum = [[Buf() for _ in range(NTG)] for _ in range(8)]
        epsb = A.alloc("epsb", [128, 8], F32)
        b_eps = Buf()
        P.dma("sp", f_dma(pcol[:, 0:PC_N], pcol_d[:, :]), writes=[b_pcol])
        P.dma("sp", f_dma(cbf[:], cbf_d[:, :]), writes=[b_cbf])
        PC_L1GA = PC_N
        PC_L1BA = PC_N + 16
        P.op("dve", f_ts(pcol[:, PC_N:PC_N + 32], pcol[:, PC_L1G:PC_L1G + 32], ALPHA, None, ALU.mult),
             reads=[b_pcol], writes=[b_pcol])
        P.op("dve", f_memset(epsb[:, 0:1], RMS_EPS), writes=[b_eps])
        P.op("dve", f_memset(epsb[:, 1:2], LN_EPS), reads=[b_eps], writes=[b_eps])
        eps_rms = epsb[:, 0:1]
        eps_ln = epsb[:, 1:2]
        ones = cbf[:, CB_ONES:CB_ONES + 128]
        ident = cbf[:, CB_ID:CB_ID + 128]
        Pm = cbf[:, CB_PM:CB_PM + 128]
        Pa = cbf[:, CB_PA:CB_PA + 128]
        gmark = A.mark()

        rr = {"ev": 0, "bank": 0}

        def evac(out_ap, in_ap, reads, writes):
            rr["ev"] += 1
            if rr["ev"] % 2 == 0:
                return P.op("act", f_act(out_ap, in_ap, AF.Copy), reads=reads, writes=writes)
            return P.op("dve", f_copy(out_ap, in_ap), reads=reads, writes=writes)

        def nbank(lo, hi):
            rr["bank"] += 1
            return lo + rr["bank"] % (hi - lo)

        if any(p in phases for p in ("A", "B", "C")):
            Wr = [A.alloc("Wr0", [128, 16, 1408], BF16), A.alloc("Wr1", [128, 16, 1408], BF16)]
            b_Wr = [Buf("Wr0"), Buf("Wr1")]
            wuq = A.alloc("wuq", [128, 3, 1536], BF16)
            wukv = A.alloc("wukv", [128, 2, 2048], BF16)
            b_wuq = Buf("wuq")
            b_wukv = Buf("wukv")
            xb = [A.alloc("xb0", [128, 16, 512], BF16), A.alloc("xb1", [128, 16, 512], BF16)]
            b_xb = [Buf("xb0"), Buf("xb1")]
            posi = A.alloc("posi", [128, 512], I32)
            posf = A.alloc("posf", [128, 512], F32)
            b_posi = Buf()
            b_posf = Buf()
            turn = A.alloc("turn", [128, 512], F32)
            kks = [A.alloc(f"kk{i}", [128, 512], F32) for i in range(4)]
            b_turn = Buf()
            b_kks = [Buf() for _ in range(4)]
            tabs2 = [[A.alloc(f"tab{k}_{i}", [128, 512], F32) for i in range(4)] for k in range(2)]
            b_tabs2 = [[Buf() for _ in range(4)] for _ in range(2)]
            cur = {"set": 0}
            NHB = 3
            hb = [A.alloc(f"hb{i}", [128, 512], BF16) for i in range(NHB)]
            b_hb = [Buf() for _ in range(NHB)]
            t1 = [A.alloc(f"t1_{i}", [128, 512], F32) for i in range(2)]
            b_t1 = [Buf() for _ in range(2)]
            t2 = [A.alloc(f"t2_{i}", [128, 512], F32) for i in range(2)]
            b_t2 = [Buf() for _ in range(2)]
            n32 = t1
            b_n32 = b_t1
            cb = [A.alloc(f"cb{i}", [128, 512], BF16) for i in range(3)]
            b_cb = [Buf() for _ in range(3)]
            sq = [A.alloc(f"sq{i}", [128, 512], BF16) for i in range(3)]
            b_sq = [Buf() for _ in range(3)]
            cn = [A.alloc(f"cn{i}", [128, 512], BF16) for i in range(3)]
            b_cn = [Buf() for _ in range(3)]
            rstd = A.alloc("rstd", [128, 512], F32)
            b_rstd = Buf()
            rt, b_rt = rstd, b_rstd
            NKS = 4
            kst = [A.alloc(f"kst{i}", [128, 512], BF16) for i in range(NKS)]
            b_kst = [Buf() for _ in range(NKS)]
            vst = [A.alloc(f"vst{i}", [128, 1024], BF16) for i in range(2)]
            b_vst = [[Buf(), Buf()] for _ in range(2)]
            cnt = {"hb": 0, "t": 0, "kst": 0, "vst": 0, "xb": 0}

            def load_w(dst_i, src, ncols):
                v = src.rearrange("(c p) n -> p c n", p=128)
                cuts = [0, 384, 768, 1088, 1408] if ncols == 1408 else [0, 256, 512, 768, 1024]
                fns = []
                for g in range(4):
                    for cg in range(2):
                        fns.append(f_dma(Wr[dst_i][:, 8 * cg:8 * cg + 8, cuts[g]:cuts[g + 1]], v[:, 8 * cg:8 * cg + 8, cuts[g]:cuts[g + 1]]))
                P.dma_multi("pool", fns, writes=[b_Wr[dst_i]])

            def load_x(src, col0):
                i = cnt["xb"] % 2
                cnt["xb"] += 1
                v = src.rearrange("(c p) t -> p c t", p=128)
                P.dma_multi("pool", [f_dma(xb[i][:, 4 * g:4 * g + 4, :], v[:, 4 * g:4 * g + 4, col0:col0 + 512]) for g in range(4)],
                            writes=[b_xb[i]])
                return i

            def make_tables(psrc, col0, tset):
                tabs, b_tabs = tabs2[tset], b_tabs2[tset]

                def dve_part(ti):
                    if ti == 0:
                        P.dma("sp", f_dma(posi[:], psrc[0:1, col0:col0 + 512].partition_broadcast(128)), writes=[b_posi])
                        P.op("dve", f_copy(posf[:], posi[:]), reads=[b_posi], writes=[b_posf])
                    ci = PC_ROPE + 2 * ti
                    kk, b_kk = kks[ti], b_kks[ti]
                    P.op("dve", f_ts(turn[:], posf[:], pcol[:, ci:ci + 1], pcol[:, ci + 1:ci + 2], ALU.mult, ALU.add),
                         reads=[b_posf, b_pcol], writes=[b_turn])
                    P.op("dve", f_ts(kk[:], turn[:], MAGIC, None, ALU.add), reads=[b_turn], writes=[b_kk])
                    P.op("dve", f_ts(kk[:], kk[:], MAGIC, None, ALU.subtract), reads=[b_kk], writes=[b_kk])
                    P.op("dve", f_tt(kk[:], turn[:], kk[:], ALU.subtract), reads=[b_turn, b_kk], writes=[b_kk])

                def sin_part():
                    for ti in range(4):
                        P.op("act", f_act(tabs[ti][:], kks[ti][:], AF.Sin, scale=TWO_PI_S), reads=[b_kks[ti]], writes=[b_tabs[ti]])
                return [(lambda ti=ti: dve_part(ti)) for ti in range(4)], sin_part

            def make_tables_now(psrc, col0, tset):
                parts, sp_ = make_tables(psrc, col0, tset)
                for p_ in parts:
                    p_()
                sp_()

            def rope_sum(ti, out_ap, out_bufs):
                P.op("pool", f_tt(out_ap, t1[ti][:], t2[ti][:], ALU.add), reads=[b_t1[ti], b_t2[ti]], writes=out_bufs)

            def next_kst():
                i = cnt["kst"] % NKS
                cnt["kst"] += 1
                return i

            def rms_part1(xi, wsrc, b_w, col0, nch):
                for j in range(nch):
                    bk = nbank(0, 5)
                    MM(P, psb[bk], ps[bk][:], [(wsrc[:, c, col0 + j * 128:col0 + (j + 1) * 128], xb[xi][:, c, :],
                                                [b_xb[xi], b_w]) for c in range(16)])
                    P.op("act", f_act(cb[j][:], ps[bk][:], AF.Copy), reads=[psb[bk]], writes=[b_cb[j]])
                    P.op("act", f_act(sq[j][:], ps[bk][:], AF.Square), reads=[psb[bk]], writes=[b_sq[j]])

            def rms_part2(nch, gcol, inv_n):
                sk = nbank(5, 8)
                MM(P, psb[sk], ps[sk][:], [(ones, sq[j][:], [b_sq[j], b_cbf]) for j in range(nch)])
                P.op("act", f_act(rt[:], ps[sk][:], AF.Sqrt, scale=inv_n, bias=eps_rms), reads=[psb[sk], b_eps], writes=[b_rt])
                P.op("dve", f_recip(rstd[:], rt[:]), reads=[b_rt], writes=[b_rstd])
                for j in range(nch):
                    P.op("dve", f_stt(cn[j][:], cb[j][:], pcol[:, gcol + j:gcol + j + 1], rstd[:], ALU.mult, ALU.mult),
                         reads=[b_cb[j], b_rstd, b_pcol], writes=[b_cn[j]])

            def rope_part1(bank, typ):
                hi = cnt["hb"] % NHB
                cnt["hb"] += 1
                P.op("act", f_act(hb[hi][:], ps[bank][:], AF.Copy), reads=[psb[bank]], writes=[b_hb[hi]])
                return (hi, typ)

            def rope_part2(state):
                hi, typ = state
                tabs, b_tabs = tabs2[cur["set"]], b_tabs2[cur["set"]]
                Pmat = Pm if typ == 0 else Pa
                Ct, Sg = tabs[2 * typ], tabs[2 * typ + 1]
                bC, bS = b_tabs[2 * typ], b_tabs[2 * typ + 1]
                ti = cnt["t"] % 2
                cnt["t"] += 1
                xbk = nbank(5, 8)
                MM(P, psb[xbk], ps[xbk][:], [(Pmat, hb[hi][:], [b_hb[hi], b_cbf])])
                P.op("pool", f_tt(t1[ti][:], hb[hi][:], Ct[:], ALU.mult), reads=[b_hb[hi], bC], writes=[b_t1[ti]])
                P.op("dve", f_tt(t2[ti][:], ps[xbk][:], Sg[:], ALU.mult), reads=[psb[xbk], bS], writes=[b_t2[ti]])
                return ti

            P.dma("pool", f_dma(wuq[:], wuq_d.rearrange("(c p) n -> p c n", p=128)), writes=[b_wuq])
            P.dma("pool", f_dma(wukv[:], wukv_d.rearrange("(c p) n -> p c n", p=128)), writes=[b_wukv])

            nxt = None
            if "A" in phases:
                nxt = load_x(xT, 0)
                load_w(0, wA, 1408)
                WA = Wr[0]
                for tg in range(NTG):
                    xi = nxt
                    if tg + 1 < NTG:
                        nxt = load_x(xT, (tg + 1) * 512)
                    elif "B" in phases:
                        load_w(1, wB, 1024)
                        nxt = load_x(xT, 0)
                    if tg == 0:
                        make_tables_now(pos, 0, 0)
                    tparts, sin_next = [], None
                    cur["set"] = tg % 2
                    c0 = tg * 512
                    pend = []

                    def up_k(h, c0=c0):
                        bk = nbank(0, 5)
                        MM(P, psb[bk], ps[bk][:], [(wukv[:, r, h * 128:(h + 1) * 128], cn[r][:], [b_cn[r], b_wukv]) for r in range(2)])
                        ki = next_kst()
                        evac(kst[ki][:], ps[bk][:], [psb[bk]], [b_kst[ki]])
                        P.dma("sp", f_dma(KTa[h, :, c0:c0 + 512], kst[ki][:]), reads=[b_kst[ki]])

                    def up_v(tt, c0=c0):
                        vi = cnt["vst"] % 2
                        cnt["vst"] += 1
                        for half in range(2):
                            bk = nbank(0, 5)
                            MM(P, psb[bk], ps[bk][:], [(cn[r][:, tt * 128:(tt + 1) * 128],
                                                        wukv[:, r, 1024 + half * 512:1024 + (half + 1) * 512],
                                                        [b_cn[r], b_wukv]) for r in range(2)])
                            evac(vst[vi][:, half * 512:(half + 1) * 512], ps[bk][:], [psb[bk]], [b_vst[vi][half]])
                        r0 = c0 + tt * 128
                        P.dma("sp", f_dma(Va[:, r0:r0 + 128, :].rearrange("h p d -> p h d"),
                                          vst[vi][:].rearrange("p (h d) -> p h d", h=8)), reads=b_vst[vi])

                    def fin_kr(state, c0=c0):
                        ti = rope_part2(state)
                        ki = next_kst()
                        rope_sum(ti, kst[ki][:], [b_kst[ki]])
                        P.dma("sp", f_dma(KR[:, c0:c0 + 512], kst[ki][:]), reads=[b_kst[ki]])

                    def fin_mk(state, h, tg=tg, c0=c0):
                        ti = rope_part2(state)
                        rope_sum(ti, n32[ti][:], [b_n32[ti]])
                        P.op("dve", f_red(ksum[:, h * 16 + 2 * tg:h * 16 + 2 * tg + 2],
                                          n32[ti][:].rearrange("p (a b) -> p a b", a=2), ALU.add),
                             reads=[b_n32[ti]], writes=[b_ksum[h][tg]])
                        ki = next_kst()
                        P.op("act", f_act(kst[ki][:], n32[ti][:], AF.Copy), reads=[b_n32[ti]], writes=[b_kst[ki]])
                        P.dma("sp", f_dma(KTm[h, :, c0:c0 + 512], kst[ki][:]), reads=[b_kst[ki]])

                    rms_part1(xi, WA, b_Wr[0], 0, 2)
                    bk = nbank(0, 5)
                    MM(P, psb[bk], ps[bk][:], [(WA[:, c, 256:384], xb[xi][:, c, :], [b_xb[xi], b_Wr[0]]) for c in range(16)])
                    st_kr = rope_part1(bk, 1)
                    rms_part2(2, PC_KVN, 1.0 / 256.0)
                    ups = [(lambda h=h: up_k(h)) for h in range(8)] + [(lambda tt=tt: up_v(tt)) for tt in range(4)]
                    prev = lambda: fin_kr(st_kr)
                    for h in range(8):
                        bk = nbank(0, 5)
                        MM(P, psb[bk], ps[bk][:], [(WA[:, c, 384 + h * 128:384 + (h + 1) * 128], xb[xi][:, c, :],
                                                    [b_xb[xi], b_Wr[0]]) for c in range(16)])
                        st_h = rope_part1(bk, 0)
                        prev()
                        prev = (lambda s_=st_h, h=h: fin_mk(s_, h))
                        if h >= 1:
                            for _ in range(2):
                                if ups:
                                    ups.pop(0)()
                        if h == 1 and tg + 1 < NTG:
                            tparts, sin_next = make_tables(pos, (tg + 1) * 512, (tg + 1) % 2)
                        if tparts:
                            tparts.pop(0)()
                        if h == 6 and sin_next is not None:
                            sin_next()
                    prev()
                    while ups:
                        ups.pop(0)()
                allks = [b for hh in b_ksum for b in hh]
                P.op("dve", f_ts(kmean[:], ksum[:], 1.0 / 256.0, None, ALU.mult), reads=allks, writes=[b_kmean])

            if "B" in phases:
                if "A" not in phases:
                    load_w(1, wB, 1024)
                    nxt = load_x(xT, 0)
                WB = Wr[1]
                cparts, sin_c = [], None
                if "C" in phases:
                    cparts, sin_c = make_tables(poso, 0, 0)
                for tg in range(NTG):
                    if cparts:
                        cparts.pop(0)()
                    if tg == 5 and sin_c is not None:
                        sin_c()
                    xi = nxt
                    if tg + 1 < NTG:
                        nxt = load_x(xT, (tg + 1) * 512)
                    elif "C" in phases:
                        load_w(0, wC, 1408)
                        nxt = load_x(xTo, 0)
                    c0 = tg * 512
                    for tt in range(4):
                        vi = cnt["vst"] % 2
                        cnt["vst"] += 1
                        for half in range(2):
                            bk = nbank(0, 8)
                            MM(P, psb[bk], ps[bk][:], [(xb[xi][:, c, tt * 128:(tt + 1) * 128],
                                                        WB[:, c, half * 512:(half + 1) * 512],
                                                        [b_xb[xi], b_Wr[1]]) for c in range(16)])
                            evac(vst[vi][:, half * 512:(half + 1) * 512], ps[bk][:], [psb[bk]], [b_vst[vi][half]])
                        r0 = c0 + tt * 128
                        P.dma("sp", f_dma(Vm[:, r0:r0 + 128, :].rearrange("h p d -> p h d"),
                                          vst[vi][:].rearrange("p (h d) -> p h d", h=8)), reads=b_vst[vi])

            if "C" in phases:
                if "B" not in phases:
                    load_w(0, wC, 1408)
                    nxt = load_x(xTo, 0)
                WC = Wr[0]
                for s in range(NSL):
                    xi = nxt
                    if s + 1 < NSL:
                        nxt = load_x(xTo, (s + 1) * 512)
                    if s == 0 and "B" not in phases:
                        make_tables_now(poso, 0, 0)
                    tparts, sin_next = [], None
                    cur["set"] = s % 2
                    c0 = s * 512
                    def up_qn(h, c0=c0):
                        bk = nbank(0, 5)
                        MM(P, psb[bk], ps[bk][:], [(wuq[:, r, h * 128:(h + 1) * 128], cn[r][:], [b_cn[r], b_wuq]) for r in range(3)])
                        ki = next_kst()
                        evac(kst[ki][:], ps[bk][:], [psb[bk]], [b_kst[ki]])
                        P.dma("sp", f_dma(QTa[h, :, c0:c0 + 512], kst[ki][:]), reads=[b_kst[ki]])

                    def up_qr1(pr):
                        bk = nbank(0, 5)
                        MM(P, psb[bk], ps[bk][:], [(wuq[:, r, 1024 + pr * 128:1024 + (pr + 1) * 128], cn[r][:], [b_cn[r], b_wuq])
                                                    for r in range(3)])
                        return rope_part1(bk, 1)

                    def fin_rope(state, dst):
                        ti = rope_part2(state)
                        ki = next_kst()
                        rope_sum(ti, kst[ki][:], [b_kst[ki]])
                        P.dma("sp", f_dma(dst, kst[ki][:]), reads=[b_kst[ki]])

                    rms_part1(xi, WC, b_Wr[0], 0, 3)
                    prev = None
                    qr_states = []
                    for h in range(8):
                        bk = nbank(0, 5)
                        MM(P, psb[bk], ps[bk][:], [(WC[:, c, 384 + h * 128:384 + (h + 1) * 128], xb[xi][:, c, :],
                                                    [b_xb[xi], b_Wr[0]]) for c in range(16)])
                        st_h = rope_part1(bk, 0)
                        if h == 0:
                            rms_part2(3, PC_QN, 1.0 / 384.0)
                        if prev is not None:
                            prev()
                        prev = (lambda s_=st_h, h=h: fin_rope(s_, QTm[h, :, c0:c0 + 512]))
                        if h >= 1:
                            up_qn(h - 1)
                        if h == 1 and s + 1 < NSL:
                            tparts, sin_next = make_tables(poso, (s + 1) * 512, (s + 1) % 2)
                        if tparts:
                            tparts.pop(0)()
                        if h == 6 and sin_next is not None:
                            sin_next()
                        if h >= 2 and h - 2 < 4:
                            pr = h - 2
                            if qr_states:
                                pr0, s0 = qr_states.pop(0)
                                fin_rope(s0, QRa[pr0, :, c0:c0 + 512])
                            qr_states.append((pr, up_qr1(pr)))
                    prev()
                    up_qn(7)
                    while qr_states:
                        pr0, s0 = qr_states.pop(0)
                        fin_rope(s0, QRa[pr0, :, c0:c0 + 512])
            P.barrier()
            A.reset(gmark)

        if "MLA" in phases or "MOBA" in phases:
            cmask = A.alloc("cmask", [128, 32, 512], BF16)
            b_cmask = Buf()
            cmv = cmask_d.rearrange("s i p q -> p (s i) q")
            P.dma_multi("sp", [f_dma(cmask[:, 8 * g:8 * g + 8, :], cmv[:, 8 * g:8 * g + 8, :]) for g in range(4)], writes=[b_cmask])
            KRe = A.alloc("KRe", [128, S], BF16)
            KRo = A.alloc("KRo", [128, S], BF16)
            b_KRe, b_KRo = Buf(), Buf()
            P.op("dve", f_memset(KRe[64:128, :], 0.0), writes=[b_KRe])
            P.op("dve", f_memset(KRo[0:64, :], 0.0), writes=[b_KRo])
            P.dma("sp", f_dma(KRe[0:64, :], KR[0:64, :]), reads=[b_KRe], writes=[b_KRe])
            P.dma("sp", f_dma(KRo[64:128, :], KR[64:128, :]), reads=[b_KRo], writes=[b_KRo])
            zf = A.alloc("zf", [128, 1024], F32)
            b_zf = Buf()
            P.op("dve", f_memset(zf[:], 0.0), writes=[b_zf])
            gcs = A.alloc("gcs", [128, 3, 256], F32)
            b_gcs = Buf()
            P.dma("sp", f_dma(gcs[:], gc_d[:, :, :]), writes=[b_gcs])
            Kh = [A.alloc(f"Kh{i}", [128, S], BF16) for i in range(2)]
            Vh = [A.alloc(f"Vh{i}", [128, 32, 128], BF16) for i in range(2)]
            Qh = [A.alloc(f"Qh{i}", [128, 2048], BF16) for i in range(2)]
            QRh = [A.alloc(f"QRh{i}", [128, 2048], BF16) for i in range(2)]
            b_Kh = [Buf(), Buf()]
            b_Vh = [Buf(), Buf()]
            b_Qh = [Buf(), Buf()]
            b_QRh = [Buf(), Buf()]
            NPT = 6
            PT = [A.alloc(f"PT{i}", [128, 1024], BF16) for i in range(NPT)]
            b_PT = [Buf() for _ in range(NPT)]
            negT = [A.alloc(f"negT{i}", [128, 512], BF16) for i in range(2)]
            b_negT = [Buf(), Buf()]
            for i in range(2):
                P.op("dve", f_memset(negT[i][:], 0.0), writes=[b_negT[i]])
            gm = [A.alloc(f"gm{i}", [128, 256], F32) for i in range(4)]
            b_gm = [Buf() for _ in range(4)]
            mx = A.alloc("mx", [128, 16], F32)
            b_mx = Buf()
            negm = [A.alloc(f"negm{i}", [128, 256], BF16) for i in range(2)]
            b_negm = [Buf(), Buf()]
            rinv = A.alloc("rinv", [128, 512], F32)
            b_rinv = Buf()
            accf = [[A.alloc(f"accf{i}{j}", [128, 1024], F32) for j in range(2)] for i in range(2)]
            b_accf = [[Buf(), Buf()] for _ in range(2)]
            accb = [A.alloc(f"accb{j}", [128, 1024], BF16) for j in range(2)]
            b_accb = [Buf(), Buf()]
            ost = [A.alloc(f"ost{i}", [128, 512], BF16) for i in range(2)]
            b_ost = [Buf(), Buf()]
            c2 = {"pt": 0, "sp": 0, "ol": 0, "ost": 0, "negT": 0}
            SP_ = [(0, 1), (2, 3)]
            OB = [4, 5]
            LBK = 6
            GB = 7

            heads = []
            if "MLA" in phases:
                heads += [("a", h) for h in range(8)]
            if "MOBA" in phases:
                heads += [("m", h) for h in range(8)]

            def load_head(idx):
                typ, h = heads[idx]
                hs = idx % 2
                Ksrc, Vsrc, Qsrc = (KTa, Va, QTa) if typ == "a" else (KTm, Vm, QTm)
                P.dma("sp", f_dma(Qh[hs][:], Qsrc[h, :, :]), writes=[b_Qh[hs]])
                P.dma("sp", f_dma(Kh[hs][:], Ksrc[h, :, :]), writes=[b_Kh[hs]])
                P.dma_multi("sp", [f_dma(Vh[hs][:, q4 * 8:(q4 + 1) * 8, :],
                                         Vsrc[h, q4 * 1024:(q4 + 1) * 1024, :].rearrange("(t p) d -> p t d", p=128))
                                   for q4 in range(4)], writes=[b_Vh[hs]])
                if typ == "a" and h % 2 == 0:
                    pi = (h // 2) % 2
                    P.dma("sp", f_dma(QRh[pi][:], QRa[h // 2, :, :]), writes=[b_QRh[pi]])

            def gating(idx):
                typ, h = heads[idx]
                hs = idx % 2
                for t in range(16):
                    P.op("pe", f_mm(ps[GB][:, t * 16:(t + 1) * 16], Qh[hs][:, t * 128:(t + 1) * 128],
                                    kmean[:, h * 16:(h + 1) * 16], True, True),
                         reads=[b_Qh[hs], b_kmean], writes=[psb[GB]], signal=(t == 15))
                g0, g1, g2, g3 = gm

                def v3(ap):
                    return ap.rearrange("p (t n) -> p t n", t=16)

                def mxb():
                    return mx[:].unsqueeze(2).to_broadcast([128, 16, 16])

                P.op("dve", f_tt(g0[:], ps[GB][:, 0:256], gcs[:, 0, :], ALU.add), reads=[psb[GB], b_gcs], writes=[b_gm[0]])
                src_, bsrc = g0, b_gm[0]
                for it in range(2):
                    P.op("dve", f_red(mx[:], v3(src_[:]), ALU.max), reads=[bsrc], writes=[b_mx])
                    P.op("dve", f_tt(v3(g3[:]), v3(src_[:]), mxb(), ALU.is_equal), reads=[bsrc, b_mx], writes=[b_gm[3]])
                    dst, bdst = (g1, b_gm[1]) if it == 0 else (g2, b_gm[2])
                    P.op("dve", f_stt(dst[:], g3[:], -1e30, src_[:], ALU.mult, ALU.add), reads=[b_gm[3], bsrc], writes=[bdst])
                    src_, bsrc = dst, bdst
                P.op("dve", f_red(mx[:], v3(g2[:]), ALU.max), reads=[b_gm[2]], writes=[b_mx])
                P.op("dve", f_tt(v3(g3[:]), v3(g0[:]), mxb(), ALU.is_ge), reads=[b_gm[0], b_mx], writes=[b_gm[3]])
                P.op("dve", f_tt(g3[:], g3[:], gcs[:, 1, :], ALU.mult), reads=[b_gm[3], b_gcs], writes=[b_gm[3]])
                P.op("dve", f_tt(g3[:], g3[:], gcs[:, 2, :], ALU.add), reads=[b_gm[3], b_gcs], writes=[b_gm[3]])
                P.op("dve", f_ts(negm[hs][:], g3[:], -NEG, NEG, ALU.mult, ALU.add), reads=[b_gm[3]], writes=[b_negm[hs]])

            pend_fin = []

            if heads:
                load_head(0)
                if heads[0][0] == "m":
                    gating(0)
            for idx, (typ, h) in enumerate(heads):
                hs = idx % 2
                if idx + 1 < len(heads):
                    load_head(idx + 1)
                scale = MLA_SCALE if typ == "a" else MOBA_SCALE
                for s in range(NSL):
                    nk = 8 * (s + 1)
                    npair = nk // 2
                    q0 = s * 512
                    ni = None
                    if typ == "m":
                        ni = c2["negT"] % 2
                        c2["negT"] += 1
                        for i4 in range(4):
                            t = 4 * s + i4
                            P.op("pe", f_mm(ps[GB][0:16, i4 * 128:(i4 + 1) * 128], negm[hs][:, t * 16:(t + 1) * 16], ident, True, True),
                                 reads=[b_negm[hs], b_cbf], writes=[psb[GB]], signal=(i4 == 3))
                        P.op("act", f_act(negT[ni][0:16, :], ps[GB][0:16, :], AF.Copy), reads=[psb[GB]], writes=[b_negT[ni]])
                    oi = c2["ol"] % 2
                    c2["ol"] += 1
                    ob = OB[oi]

                    def emit_qk_pair(pp):
                        b0, b1 = SP_[c2["sp"] % 2]
                        c2["sp"] += 1
                        for j, bk in enumerate((b0, b1)):
                            kt = 2 * pp + j
                            items = [(Kh[hs][:, kt * 128:(kt + 1) * 128], Qh[hs][:, q0:q0 + 512], [b_Kh[hs], b_Qh[hs]])]
                            if typ == "a":
                                pi = (h // 2) % 2
                                KRx, b_KRx = (KRe, b_KRe) if h % 2 == 0 else (KRo, b_KRo)
                                items.append((KRx[:, kt * 128:(kt + 1) * 128], QRh[pi][:, q0:q0 + 512], [b_KRx, b_QRh[pi]]))
                            else:
                                n = kt // 2
                                items.append((cbf[:, CB_E + n * 128:CB_E + (n + 1) * 128], negT[ni][:], [b_cbf, b_negT[ni]]))
                            if kt >= nk - 8:
                                items.append((ident, cmask[:, s * 8 + kt - (nk - 8), :], [b_cbf, b_cmask]))
                            MM(P, psb[bk], ps[bk][:], items)
                        return (b0, b1)

                    banks = [emit_qk_pair(0), emit_qk_pair(1)]
                    for pp in range(npair):
                        b0, b1 = banks[pp]
                        pi_ = c2["pt"] % NPT
                        c2["pt"] += 1
                        P.op("act", f_act(PT[pi_][:], psall[:, b0 * 512:b0 * 512 + 1024], AF.Exp, scale=scale),
                             reads=[psb[b0], psb[b1]], writes=[b_PT[pi_]])
                        if pp + 2 < npair:
                            banks.append(emit_qk_pair(pp + 2))
                        for j in range(2):
                            kt = 2 * pp + j
                            first, last = kt == 0, kt == nk - 1
                            P.op("pe", f_mm(ps[ob][:], Vh[hs][:, kt, :], PT[pi_][:, j * 512:(j + 1) * 512], first, last),
                                 reads=[b_PT[pi_], b_Vh[hs]], writes=([psb[ob]] if (first or last) else []), signal=(j == 1))
                        ai = 0
                        aeng = "dve"
                        if pp < 1:
                            P.op(aeng, f_tt(accf[oi][ai][:], zf[:], PT[pi_][:], ALU.add), reads=[b_PT[pi_], b_zf], writes=[b_accf[oi][ai]])
                        else:
                            P.op(aeng, f_tt(accf[oi][ai][:], accf[oi][ai][:], PT[pi_][:], ALU.add),
                                 reads=[b_PT[pi_], b_accf[oi][ai]], writes=[b_accf[oi][ai]])
                        if pp >= 1 and pend_fin:
                            pend_fin.pop(0)()
                        if pp == 1 and s == 1 and idx + 1 < len(heads) and heads[idx + 1][0] == "m":
                            gating(idx + 1)

                    while len(pend_fin) > 0:
                        pend_fin.pop(0)()

                    def fin1(oi=oi):
                        P.op("act", f_act(accb[0][:], accf[oi][0][:], AF.Copy), reads=[b_accf[oi][0]], writes=[b_accb[0]])

                    def fin2():
                        MM(P, psb[LBK], ps[LBK][:], [(ones, accb[0][:, hh * 512:(hh + 1) * 512], [b_accb[0], b_cbf])
                                                      for hh in range(2)])

                    def fin3():
                        P.op("act", f_act(rinv[:], ps[LBK][:], AF.Ln), reads=[psb[LBK]], writes=[b_rinv])
                        P.op("act", f_act(rinv[:], rinv[:], AF.Exp, scale=-1.0), reads=[b_rinv], writes=[b_rinv])

                    def fin4(ob=ob, q0=q0, ch=(h if typ == "a" else 8 + h)):
                        oi2 = c2["ost"] % 2
                        c2["ost"] += 1
                        P.op("dve", f_tt(ost[oi2][:], ps[ob][:], rinv[:], ALU.mult), reads=[psb[ob], b_rinv], writes=[b_ost[oi2]])
                        P.dma("sp", f_dma(AT[ch, :, q0:q0 + 512], ost[oi2][:]), reads=[b_ost[oi2]])

                    pend_fin.extend([fin1, fin2, fin3, fin4])
            while pend_fin:
                pend_fin.pop(0)()
            P.barrier()
            A.reset(gmark)

        if "POST" in phases:
            R = A.alloc("R", [128, 16, 1024], F32)
            b_R = [[Buf(), Buf()] for _ in range(16)]
            x1T = A.alloc("x1T", [128, 16, 1024], BF16)
            b_x1 = [[Buf(), Buf()] for _ in range(16)]
            asl = A.alloc("asl", [128, 16, 512], BF16)
            b_asl = [Buf() for _ in range(16)]
            hid = A.alloc("hid", [128, 8, 1024], BF16)
            b_hid = [[Buf(), Buf()] for _ in range(8)]
            hid_ln = hid[:].rearrange("p k (a t) -> p (k a) t", a=2)
            NW = 4
            wt = [A.alloc(f"wt{i}", [128, 16, 128], BF16) for i in range(NW)]
            b_wt = [Buf() for _ in range(NW)]
            NWD = 6
            wd = [A.alloc(f"wd{i}", [128, 8, 128], BF16) for i in range(NWD)]
            b_wd = [Buf() for _ in range(NWD)]
            rl = [A.alloc(f"rl{i}", [128, 512], F32) for i in range(2)]
            b_rl = [Buf(), Buf()]
            mean = A.alloc("mean", [128, 512], F32)
            msq = A.alloc("msq", [128, 512], F32)
            var = A.alloc("var", [128, 512], F32)
            rs2 = A.alloc("rs2", [128, 512], F32)
            nmr = A.alloc("nmr", [128, 512], F32)
            b_mean, b_msq, b_var, b_rs2, b_nmr = Buf(), Buf(), Buf(), Buf(), Buf()
            tA = [A.alloc(f"tA{i}", [128, 512], F32) for i in range(2)]
            b_tA = [Buf(), Buf()]
            tB = [A.alloc(f"tB{i}", [128, 512], F32) for i in range(2)]
            b_tB = [Buf(), Buf()]
            c3 = {"wt": 0, "wd": 0, "rl": 0, "t": 0, "bank": 0}

            def pbank():
                c3["bank"] += 1
                return c3["bank"] % 6

            def layer_norm(half, gcol, bcol, first):
                hs_ = slice(half * 512, (half + 1) * 512)
                for c in range(16):
                    P.op("act", f_act(asl[:, c, :], R[:, c, hs_], AF.Copy), reads=[b_R[c][half]], writes=[b_asl[c]])
                    P.op("dve", f_tt(hid_ln[:, c, :], R[:, c, hs_], R[:, c, hs_], ALU.mult), reads=[b_R[c][half]], writes=[b_hid[c // 2][c % 2]])
                MM(P, psb[6], ps[6][:], [(ones, asl[:, c, :], [b_asl[c], b_cbf]) for c in range(16)])
                MM(P, psb[7], ps[7][:], [(ones, hid_ln[:, c, :], [b_hid[c // 2][c % 2], b_cbf]) for c in range(16)])
                P.op("dve", f_ts(mean[:], ps[6][:], 1.0 / D, None, ALU.mult), reads=[psb[6]], writes=[b_mean])
                P.op("dve", f_tt(msq[:], mean[:], mean[:], ALU.mult), reads=[b_mean], writes=[b_msq])
                P.op("dve", f_stt(var[:], ps[7][:], 1.0 / D, msq[:], ALU.mult, ALU.subtract), reads=[psb[7], b_msq], writes=[b_var])
                P.op("act", f_act(var[:], var[:], AF.Sqrt, scale=1.0, bias=eps_ln), reads=[b_var, b_eps], writes=[b_var])
                P.op("dve", f_recip(rs2[:], var[:]), reads=[b_var], writes=[b_rs2])
                P.op("dve", f_stt(nmr[:], mean[:], -1.0, rs2[:], ALU.mult, ALU.mult), reads=[b_mean, b_rs2], writes=[b_nmr])
                for c in range(16):
                    i = c3["t"] % 2
                    c3["t"] += 1
                    P.op("dve", f_tt(tA[i][:], R[:, c, hs_], rs2[:], ALU.mult), reads=[b_R[c][half], b_rs2], writes=[b_tA[i]])
                    P.op("dve", f_tt(tB[i][:], tA[i][:], nmr[:], ALU.add), reads=[b_tA[i], b_nmr], writes=[b_tB[i]])
                    if first:
                        P.op("act", f_act(x1T[:, c, hs_], tB[i][:], AF.Identity, scale=pcol[:, gcol + c:gcol + c + 1],
                                          bias=pcol[:, bcol + c:bcol + c + 1]), reads=[b_tB[i], b_pcol], writes=[b_x1[c][half]])
                        P.op("act", f_act(R[:, c, hs_], tB[i][:], AF.Identity, scale=pcol[:, PC_L1GA + c:PC_L1GA + c + 1],
                                          bias=pcol[:, PC_L1BA + c:PC_L1BA + c + 1]), reads=[b_tB[i], b_pcol], writes=[b_R[c][half]])
                    else:
                        P.op("act", f_act(R[:, c, hs_], tB[i][:], AF.Identity, scale=pcol[:, gcol + c:gcol + c + 1],
                                          bias=pcol[:, bcol + c:bcol + c + 1]), reads=[b_tB[i], b_pcol], writes=[b_R[c][half]])

            xov = xTo.rearrange("(c p) t -> p c t", p=128)
            yv = yT.rearrange("(c p) t -> p c t", p=128)
            atv = AT.rearrange("c p t -> p c t")
            for bl in range(2):
                asrc = [asl, hid_ln]
                b_asrc = [b_asl, [b_hid[c // 2][c % 2] for c in range(16)]]
                for half in range(2):
                    s = 2 * bl + half
                    q0 = s * 512
                    hs_ = slice(half * 512, (half + 1) * 512)
                    P.dma_multi("sp", [f_dma(asrc[half][:, 4 * g:4 * g + 4, :], atv[:, 4 * g:4 * g + 4, q0:q0 + 512]) for g in range(4)],
                                writes=b_asrc[half])
                    P.dma_multi("sp", [f_dma(R[:, 4 * g:4 * g + 4, hs_], xov[:, 4 * g:4 * g + 4, q0:q0 + 512]) for g in range(4)],
                                writes=[b_R[c][half] for c in range(16)])
                for dc in range(16):
                    wi = c3["wt"] % NW
                    c3["wt"] += 1
                    P.dma("pool", f_dma(wt[wi][:], wout_t[dc, :, :, :]), writes=[b_wt[wi]])
                    for half in range(2):
                        hs_ = slice(half * 512, (half + 1) * 512)
                        bk = pbank()
                        MM(P, psb[bk], ps[bk][:], [(wt[wi][:, c, :], asrc[half][:, c, :], [b_wt[wi], b_asrc[half][c]]) for c in range(16)])
                        P.op("dve", f_stt(R[:, dc, hs_], R[:, dc, hs_], ALPHA, ps[bk][:], ALU.mult, ALU.add),
                             reads=[psb[bk], b_R[dc][half]], writes=[b_R[dc][half]])
                for half in range(2):
                    layer_norm(half, PC_L1G, PC_L1B, True)
                for fb in range(8):
                    for fc in range(8):
                        wi = c3["wt"] % NW
                        c3["wt"] += 1
                        P.dma("pool", f_dma(wt[wi][:], wup_t[fb * 8 + fc, :, :, :]), writes=[b_wt[wi]])
                        for half in range(2):
                            hs_ = slice(half * 512, (half + 1) * 512)
                            bk = pbank()
                            MM(P, psb[bk], ps[bk][:], [(wt[wi][:, c, :], x1T[:, c, hs_], [b_wt[wi], b_x1[c][half]]) for c in range(16)])
                            ri = c3["rl"] % 2
                            c3["rl"] += 1
                            P.op("act", f_act(rl[ri][:], ps[bk][:], AF.Relu), reads=[psb[bk]], writes=[b_rl[ri]])
                            P.op("dve", f_tt(hid[:, fc, hs_], rl[ri][:], rl[ri][:], ALU.mult), reads=[b_rl[ri]], writes=[b_hid[fc][half]])
                    for dc in range(16):
                        wi = c3["wd"] % NWD
                        c3["wd"] += 1
                        P.dma("pool", f_dma(wd[wi][:], wdn_t[fb, dc, :, :, :]), writes=[b_wd[wi]])
                        for half in range(2):
                            hs_ = slice(half * 512, (half + 1) * 512)
                            bk = pbank()
                            MM(P, psb[bk], ps[bk][:], [(wd[wi][:, k, :], hid[:, k, hs_], [b_wd[wi], b_hid[k][half]]) for k in range(8)])
                            P.op("dve", f_tt(R[:, dc, hs_], R[:, dc, hs_], ps[bk][:], ALU.add),
                                 reads=[psb[bk], b_R[dc][half]], writes=[b_R[dc][half]])
                for half in range(2):
                    s = 2 * bl + half
                    q0 = s * 512
                    hs_ = slice(half * 512, (half + 1) * 512)
                    layer_norm(half, PC_L2G, PC_L2B, False)
                    P.dma_multi("sp", [f_dma(yv[:, 4 * g:4 * g + 4, q0:q0 + 512], R[:, 4 * g:4 * g + 4, hs_]) for g in range(4)],
                                reads=[b_R[c][half] for c in range(16)])

        for eng in ENGS:
            P.final_wait(eng)
        with nc.Block() as block:
            P.emit(block)
    return nc


def _rope_cols():
    cols = np.zeros((128, 8), np.float32)
    f16 = (THETA ** (-(np.arange(16, dtype=np.float32)) * np.float32(2.0 / 32))).astype(np.float32)
    inv_m = np.zeros(128, np.float32)
    inv_m[0:16] = f16
    inv_m[16:32] = f16
    inv_m = (inv_m.astype(np.float64) / (2 * math.pi)).astype(np.float32)
    cols[:, 0] = inv_m
    cols[:, 1] = 0.25
    cols[:, 2] = inv_m
    offs = np.zeros(128, np.float32)
    offs[0:16] = 0.5
    cols[:, 3] = offs
    f32_ = (THETA ** (-(np.arange(32, dtype=np.float32)) * np.float32(2.0 / 64))).astype(np.float32)
    inv_a = np.concatenate([f32_, f32_, f32_, f32_]).astype(np.float64) / (2 * math.pi)
    cols[:, 4] = inv_a.astype(np.float32)
    cols[:, 5] = 0.25
    cols[:, 6] = inv_a.astype(np.float32)
    offa = np.zeros(128, np.float32)
    offa[0:32] = 0.5
    offa[64:96] = 0.5
    cols[:, 7] = offa
    return cols


def _const_bf16():
    c = np.zeros((128, CB_N), np.float32)
    c[:, CB_ONES:CB_ONES + 128] = 1.0
    c[:, CB_ID:CB_ID + 128] = np.eye(128, dtype=np.float32)
    pm = np.zeros((128, 128), np.float32)
    for m in range(32):
        src = m + 16 if m < 16 else m - 16
        pm[src, m] = 1.0
    c[:, CB_PM:CB_PM + 128] = pm
    pa = np.zeros((128, 128), np.float32)
    for m in range(128):
        src = m + 32 if (m % 64) < 32 else m - 32
        pa[src, m] = 1.0
    c[:, CB_PA:CB_PA + 128] = pa
    for n in range(16):
        c[n, CB_E + n * 128:CB_E + (n + 1) * 128] = 1.0
    return c.astype(ml_dtypes.bfloat16)


def _core_consts(j):
    cm = np.zeros((4, 8, 128, 512), np.float32)
    gc = np.zeros((128, 3, 256), np.float32)
    for s in range(4):
        G = OWN[j][s]
        nk = 8 * (s + 1)
        qpos = G * 512 + np.arange(512)
        for i in range(8):
            kt = nk - 8 + i
            kpos = kt * 128 + np.arange(128)
            cm[s, i] = np.where(kpos[:, None] <= qpos[None, :], 0.0, NEG)
        for i4 in range(4):
            t = 4 * s + i4
            gt = G * 4 + i4
            qblk = gt // 2
            n = np.arange(16)
            gc[:, 0, t * 16:(t + 1) * 16] = np.where(n < qblk, 0.0, -1e30)[None, :]
            gc[:, 1, t * 16:(t + 1) * 16] = (n < qblk).astype(np.float32)[None, :]
            gc[:, 2, t * 16:(t + 1) * 16] = (n == qblk).astype(np.float32)[None, :]
    return cm.astype(ml_dtypes.bfloat16), gc


def make_in_maps(x, positions, w_in, mla_q_norm, mla_kv_norm, w_uq, w_ukv, w_out,
                 ln1_g, ln1_b, w_up, w_down, ln2_g, ln2_b):
    f = np.float32
    w_in = np.asarray(w_in[0], f)
    cq, ckv, kr, mq, mk, mv = np.split(w_in, np.cumsum([384, 256, 64, 1024, 1024])[:], axis=1)
    wA = np.ascontiguousarray(np.concatenate([ckv, kr, kr, mk], axis=1))
    wB = np.ascontiguousarray(mv)
    wC = np.ascontiguousarray(np.concatenate([cq, mq], axis=1))
    wuq0 = np.asarray(w_uq[0], f).reshape(384, 8, 192)
    wuq = np.ascontiguousarray(np.concatenate([wuq0[:, :, :128].reshape(384, 1024), wuq0[:, :, 128:].reshape(384, 512)], axis=1))
    wukv0 = np.asarray(w_ukv[0], f).reshape(256, 8, 256)
    wukv = np.ascontiguousarray(np.concatenate([wukv0[:, :, :128].reshape(256, 1024), wukv0[:, :, 128:].reshape(256, 1024)], axis=1))
    wo = np.asarray(w_out[0], f)
    wout_t = np.ascontiguousarray(wo.reshape(16, 128, 16, 128).transpose(2, 1, 0, 3))
    wu = np.asarray(w_up[0], f)
    wup_t = np.ascontiguousarray(wu.reshape(16, 128, 64, 128).transpose(2, 1, 0, 3))
    wdn = np.asarray(w_down[0], f)
    wdn_t = np.ascontiguousarray(wdn.reshape(8, 8, 128, 16, 128).transpose(0, 3, 2, 1, 4))
    pcol = np.zeros((128, PC_N), f)
    pcol[:, PC_QN:PC_QN + 3] = np.asarray(mla_q_norm[0], f).reshape(3, 128).T
    pcol[:, PC_KVN:PC_KVN + 2] = np.asarray(mla_kv_norm[0], f).reshape(2, 128).T
    pcol[:, PC_L1G:PC_L1G + 16] = np.asarray(ln1_g[0], f).reshape(16, 128).T
    pcol[:, PC_L1B:PC_L1B + 16] = np.asarray(ln1_b[0], f).reshape(16, 128).T
    pcol[:, PC_L2G:PC_L2G + 16] = np.asarray(ln2_g[0], f).reshape(16, 128).T
    pcol[:, PC_L2B:PC_L2B + 16] = np.asarray(ln2_b[0], f).reshape(16, 128).T
    pcol[:, PC_ROPE:PC_ROPE + 8] = _rope_cols()
    cbf = _const_bf16()
    cc = [_core_consts(0), _core_consts(1)]
    x = np.asarray(x, f)
    positions = np.asarray(positions, np.int32)
    in_maps = []
    for core in range(8):
        b, j = core // 2, core % 2
        xTb = np.ascontiguousarray(x[b].T)
        cols = np.concatenate([np.arange(G * 512, (G + 1) * 512) for G in OWN[j]])
        in_maps.append({
            "xT": xTb, "xTo": np.ascontiguousarray(xTb[:, cols]),
            "pos": np.ascontiguousarray(positions[b][None, :]), "poso": np.ascontiguousarray(positions[b][cols][None, :]),
            "wA": wA, "wB": wB, "wC": wC, "wuq": wuq, "wukv": wukv,
            "wout_t": wout_t, "wup_t": wup_t, "wdn_t": wdn_t,
            "pcol": pcol, "cbf": cbf, "cmask": cc[j][0], "gc": cc[j][1],
        })
    return in_maps


_NC_CACHE = {}


def kernel(**inputs):
    in_maps = make_in_maps(**inputs)
    if "nc" not in _NC_CACHE:
        _NC_CACHE["nc"] = build_program()
    nc = _NC_CACHE["nc"]
    res = run_bass_kernel_spmd(nc, in_maps, core_ids=list(range(8)))
    out = np.zeros((NB, S, D), np.float32)
    for core in range(8):
        b, j = core // 2, core % 2
        yT = np.asarray(res.results[core]["yT"])
        cols = np.concatenate([np.arange(G * 512, (G + 1) * 512) for G in OWN[j]])
        out[b, cols, :] = yT.T
    return out
```
